# Optimizing a Trainium2 kernel written in Bass

```python
import jax
import jax.numpy as jnp
from jax import lax
import numpy as np


D_MODEL = 1024
BATCH = 8
SEQ = 4096
DEPTH = 4

GRID_W = 64
CTX_LEN = 256
EPS = 1e-6
F_MIN = 1e-6
A_HEAD_DIM = 128
A_HEADS = D_MODEL // (2 * A_HEAD_DIM)
A_WIDTH = A_HEADS * A_HEAD_DIM
HGRN_CHUNK = 64
B_GROUPS = 4
B_WIDTH = D_MODEL - A_WIDTH
B_GROUP_DIM = B_WIDTH // B_GROUPS
B_CHUNK = 128
EV_IN = 5 * A_WIDTH + 2 * B_WIDTH
C_WIDTH = D_MODEL // 4
C_GROUPS = 4
C_GROUP_DIM = C_WIDTH // C_GROUPS
MLA_V_DIM = 128
MLA_HEADS = (D_MODEL - C_WIDTH) // MLA_V_DIM
MLA_NOPE = 128
MLA_ROPE = 64
MLA_QK = MLA_NOPE + MLA_ROPE
Q_LORA = 384
KV_LORA = 256
OD_IN = C_WIDTH + Q_LORA + KV_LORA + MLA_ROPE
ROPE_THETA = 10000.0
ATTN_BLOCK = 128
D_FF = 2816
CONV_W = 3

kernel_name = 'hybrid_flow_backbone'


def rms_norm(x, w):
    xf = x.astype(jnp.float32)
    y = xf * lax.rsqrt(jnp.mean(xf * xf, axis=-1, keepdims=True) + EPS)
    return y.astype(x.dtype) * w


def modulate(x, norm_w, shift, scale):
    return rms_norm(x, norm_w) * (1 + scale) + shift


def to_heads(a, n_heads):
    bsz, t_len, width = a.shape
    return a.reshape(bsz, t_len, n_heads, width // n_heads).transpose(0, 2, 1, 3)


def flip_t(a):
    return a[:, :, ::-1]


def gla_chunked(q, k, v, log_f, s0):
    bsz, nh, t_len, _ = q.shape
    dv = v.shape[-1]
    n_chunks = t_len // HGRN_CHUNK

    def to_chunks(a):
        return a.reshape(bsz, nh, n_chunks, HGRN_CHUNK, a.shape[-1]).transpose(2, 0, 1, 3, 4)

    prefix_mask = jnp.tril(jnp.ones((HGRN_CHUNK, HGRN_CHUNK), dtype=bool))[:, :, None]

    def step(state, inp):
        qc, kc, vc, gc = inp
        b = jnp.cumsum(gc, axis=2)
        diff = b[:, :, :, None, :] - b[:, :, None, :, :]
        decay = jnp.where(prefix_mask, jnp.exp(jnp.minimum(diff, 0.0)), 0.0)
        scores = jnp.einsum('bhtk,bhsk,bhtsk->bhts', qc, kc, decay)
        out = (jnp.einsum('bhtk,bhkv->bhtv', qc * jnp.exp(b), state)
               + jnp.einsum('bhts,bhsv->bhtv', scores, vc))
        b_last = b[:, :, -1:, :]
        new_state = (jnp.exp(b_last[:, :, 0, :])[..., None] * state
                     + jnp.einsum('bhsk,bhsv->bhkv', kc * jnp.exp(b_last - b), vc))
        return new_state, out

    s_fin, o = lax.scan(step, s0, (to_chunks(q), to_chunks(k), to_chunks(v), to_chunks(log_f)))
    return o.transpose(1, 2, 0, 3, 4).reshape(bsz, nh, t_len, dv), s_fin


def hgrn2_features(p, lb):
    f32 = jnp.float32
    q = to_heads(jax.nn.silu(p[..., :A_WIDTH]).astype(f32), A_HEADS)
    v = to_heads(p[..., A_WIDTH:2 * A_WIDTH].astype(f32), A_HEADS)
    out = []
    for d in range(2):
        z = p[..., (2 + d) * A_WIDTH:(3 + d) * A_WIDTH].astype(f32)
        f = jnp.maximum(lb[d] + (1.0 - lb[d]) * jax.nn.sigmoid(z), F_MIN)
        log_f = jnp.log(f)
        k = 1.0 - f
        out.append((q, to_heads(k, A_HEADS), v, to_heads(log_f, A_HEADS)))
    return out


def hgrn2_readout(o, g, onorm_w):
    bsz, _, t_len, _ = o.shape
    o = rms_norm(o.transpose(0, 2, 1, 3), onorm_w).reshape(bsz, t_len, A_WIDTH)
    return (o * jax.nn.silu(g.astype(jnp.float32))).astype(g.dtype)


def hgrn2_mixer(pl, pc, lb, onorm_w, with_ctx):
    fwd_l, bwd_l = hgrn2_features(pl, lb)
    fwd_c, bwd_c = hgrn2_features(pc, lb)
    s0 = jnp.zeros((pl.shape[0], A_HEADS, A_HEAD_DIM, A_HEAD_DIM), jnp.float32)
    oc_f, sc_f = gla_chunked(*fwd_c, s0)
    ol_f, _ = gla_chunked(*fwd_l, sc_f)
    oc_b, sc_b = gla_chunked(*[flip_t(a) for a in bwd_c], s0)
    ol_b, _ = gla_chunked(*[flip_t(a) for a in bwd_l], sc_b)
    g_sl = slice(4 * A_WIDTH, 5 * A_WIDTH)
    yl = hgrn2_readout(ol_f + flip_t(ol_b), pl[..., g_sl], onorm_w)
    yc = None
    if with_ctx:
        yc = hgrn2_readout(oc_f + flip_t(oc_b), pc[..., g_sl], onorm_w)
    return yl, yc


def chunk_mlp(u_pre, v_pre, vnorm_w, ws, bs):
    bsz, t_len, _ = u_pre.shape
    n = t_len // B_CHUNK
    u = jax.nn.gelu(u_pre)
    v = rms_norm(jax.nn.gelu(v_pre).reshape(bsz, n, B_CHUNK, B_GROUPS, B_GROUP_DIM), vnorm_w)
    sv = jnp.einsum('gts,bnsgc->bntgc', ws, v) + bs.T[:, :, None]
    return u * sv.reshape(bsz, t_len, B_WIDTH)


def even_mixer(hl, hc, w_in, lb, onorm_w, vnorm_w, ws, bs, w_out, with_ctx):
    pl = hl @ w_in
    pc = hc @ w_in
    a_sl = slice(0, 5 * A_WIDTH)
    u_sl = slice(5 * A_WIDTH, 5 * A_WIDTH + B_WIDTH)
    v_sl = slice(5 * A_WIDTH + B_WIDTH, EV_IN)
    al, ac = hgrn2_mixer(pl[..., a_sl], pc[..., a_sl], lb, onorm_w, with_ctx)
    yl = jnp.concatenate([al, chunk_mlp(pl[..., u_sl], pl[..., v_sl], vnorm_w, ws, bs)], -1) @ w_out
    yc = None
    if with_ctx:
        yc = jnp.concatenate([ac, chunk_mlp(pc[..., u_sl], pc[..., v_sl], vnorm_w, ws, bs)], -1) @ w_out
    return yl, yc


def fourier_mix(p):
    bsz, t_len, _ = p.shape
    xg = p.reshape(bsz, t_len, C_GROUPS, C_GROUP_DIM).astype(jnp.float32)
    y = jnp.fft.fft2(xg, axes=(1, 3), norm='ortho').real
    return y.reshape(bsz, t_len, C_WIDTH).astype(p.dtype)


def rope_2d(x, cos, sin):
    xr = x.reshape(*x.shape[:-1], 2, 2, MLA_ROPE // 4)
    x1, x2 = xr[..., 0, :], xr[..., 1, :]
    return jnp.stack([x1 * cos - x2 * sin, x2 * cos + x1 * sin], axis=-2).reshape(x.shape)


def mla_project(p, qa_w, w_qb, kva_w, w_kvb, qn_w, kn_w, cos, sin):
    bsz, t_len, _ = p.shape
    q_lat = rms_norm(p[..., :Q_LORA], qa_w)
    kv_lat = rms_norm(p[..., Q_LORA:Q_LORA + KV_LORA], kva_w)
    k_pe = rms_norm(p[..., Q_LORA + KV_LORA:], kn_w[MLA_NOPE:])
    q = (q_lat @ w_qb).reshape(bsz, t_len, MLA_HEADS, MLA_QK)
    kv = (kv_lat @ w_kvb).reshape(bsz, t_len, MLA_HEADS, MLA_NOPE + MLA_V_DIM)
    q_nope = rms_norm(q[..., :MLA_NOPE], qn_w[:MLA_NOPE])
    q_pe = rms_norm(q[..., MLA_NOPE:], qn_w[MLA_NOPE:])
    k_nope = rms_norm(kv[..., :MLA_NOPE], kn_w[:MLA_NOPE])
    if cos is not None:
        q_pe = rope_2d(q_pe, cos[:, None], sin[:, None])
        k_pe = rope_2d(k_pe, cos, sin)
    k_pe = jnp.broadcast_to(k_pe[:, :, None, :], (bsz, t_len, MLA_HEADS, MLA_ROPE))
    q = jnp.concatenate([q_nope, q_pe], -1)
    k = jnp.concatenate([k_nope, k_pe], -1)
    return q, k, kv[..., MLA_NOPE:]


def attend(q, k, v):
    bsz, tq, nh, dq = q.shape
    nb = tq // ATTN_BLOCK
    qb = q.reshape(bsz, nb, ATTN_BLOCK, nh, dq).transpose(1, 0, 2, 3, 4)
    scale = dq ** -0.5

    def one_block(qblk):
        s = jnp.einsum('bqhd,bkhd->bhqk', qblk, k).astype(jnp.float32) * scale
        p = jax.nn.softmax(s, axis=-1).astype(v.dtype)
        return jnp.einsum('bhqk,bkhd->bqhd', p, v)

    o = lax.map(one_block, qb)
    return o.transpose(1, 0, 2, 3, 4).reshape(bsz, tq, nh * v.shape[-1])


def odd_mixer(hl, hc, w_in, qa_w, w_qb, kva_w, w_kvb, qn_w, kn_w, w_out, cos, sin, with_ctx):
    pl = hl @ w_in
    pc = hc @ w_in
    ql, kl, vl = mla_project(pl[..., C_WIDTH:], qa_w, w_qb, kva_w, w_kvb, qn_w, kn_w, cos, sin)
    qc, kc, vc = mla_project(pc[..., C_WIDTH:], qa_w, w_qb, kva_w, w_kvb, qn_w, kn_w, None, None)
    k_all = jnp.concatenate([kc, kl], axis=1)
    v_all = jnp.concatenate([vc, vl], axis=1)
    yl = jnp.concatenate([fourier_mix(pl[..., :C_WIDTH]), attend(ql, k_all, v_all)], -1) @ w_out
    yc = None
    if with_ctx:
        yc = jnp.concatenate([fourier_mix(pc[..., :C_WIDTH]), attend(qc, kc, vc)], -1) @ w_out
    return yl, yc


def conv_ffn(h, w_up, conv_w, conv_b, w_down):
    t_len = h.shape[1]
    up = h @ w_up
    gate, val = up[..., :D_FF], up[..., D_FF:]
    pad = CONV_W // 2
    gp = jnp.pad(gate, ((0, 0), (pad, pad), (0, 0)))
    gate_c = conv_b
    for j in range(CONV_W):
        gate_c = gate_c + gp[:, j:j + t_len] * conv_w[j]
    return (jax.nn.silu(gate_c) * val) @ w_down


def setup_inputs(seed: int = 0) -> dict:
    key = jax.random.key(seed)
    ks = jax.random.split(key, 27)
    f32 = jnp.float32
    n_ev = (DEPTH + 1) // 2
    n_od = DEPTH // 2

    def normal(k, shape, scale):
        return jax.random.normal(k, shape, f32) * scale

    def gain(k, shape):
        return 1.0 + 0.02 * jax.random.normal(k, shape, f32)

    return {
        'x': normal(ks[0], (BATCH, SEQ, D_MODEL), 1.0),
        'c': normal(ks[1], (BATCH, D_MODEL), 1.0),
        'ctx': normal(ks[2], (BATCH, CTX_LEN, D_MODEL), 1.0),
        'c_ctx': normal(ks[3], (D_MODEL,), 1.0),
        'ada_w': normal(ks[4], (DEPTH, D_MODEL, 6 * D_MODEL), 0.5 * D_MODEL ** -0.5),
        'ada_b': normal(ks[5], (DEPTH, 6 * D_MODEL), 0.02),
        'norm_mix_w': gain(ks[6], (DEPTH, D_MODEL)),
        'norm_ffn_w': gain(ks[7], (DEPTH, D_MODEL)),
        'ev_w_in': normal(ks[8], (n_ev, D_MODEL, EV_IN), D_MODEL ** -0.5),
        'ev_lb_logits': normal(ks[9], (n_ev, 2, A_WIDTH), 1.0),
        'ev_onorm_w': gain(ks[10], (n_ev, A_HEAD_DIM)),
        'ev_vnorm_w': gain(ks[11], (n_ev, B_GROUPS, B_GROUP_DIM)),
        'ev_ws': normal(ks[12], (n_ev, B_GROUPS, B_CHUNK, B_CHUNK), B_CHUNK ** -0.5),
        'ev_bs': 1.0 + normal(ks[13], (n_ev, B_GROUPS, B_CHUNK), 0.1),
        'ev_w_out': normal(ks[14], (n_ev, D_MODEL, D_MODEL), D_MODEL ** -0.5),
        'od_w_in': normal(ks[15], (n_od, D_MODEL, OD_IN), D_MODEL ** -0.5),
        'od_qa_norm_w': gain(ks[16], (n_od, Q_LORA)),
        'od_w_qb': normal(ks[17], (n_od, Q_LORA, MLA_HEADS * MLA_QK), Q_LORA ** -0.5),
        'od_kva_norm_w': gain(ks[18], (n_od, KV_LORA)),
        'od_w_kvb': normal(ks[19], (n_od, KV_LORA, MLA_HEADS * (MLA_NOPE + MLA_V_DIM)), KV_LORA ** -0.5),
        'od_q_norm_w': gain(ks[20], (n_od, MLA_QK)),
        'od_k_norm_w': gain(ks[21], (n_od, MLA_QK)),
        'od_w_out': normal(ks[22], (n_od, D_MODEL, D_MODEL), D_MODEL ** -0.5),
        'ffn_w_up': normal(ks[23], (DEPTH, D_MODEL, 2 * D_FF), D_MODEL ** -0.5),
        'ffn_conv_w': normal(ks[24], (DEPTH, CONV_W, D_FF), CONV_W ** -0.5),
        'ffn_conv_b': normal(ks[25], (DEPTH, D_FF), 0.02),
        'ffn_w_down': normal(ks[26], (DEPTH, D_FF, D_MODEL), D_FF ** -0.5),
    }


def reference(x, c, ctx, c_ctx, ada_w, ada_b, norm_mix_w, norm_ffn_w, ev_w_in, ev_lb_logits,
              ev_onorm_w, ev_vnorm_w, ev_ws, ev_bs, ev_w_out, od_w_in, od_qa_norm_w, od_w_qb,
              od_kva_norm_w, od_w_kvb, od_q_norm_w, od_k_norm_w, od_w_out, ffn_w_up, ffn_conv_w,
              ffn_conv_b, ffn_w_down):
    f32 = jnp.float32
    n_lat = x.shape[1]
    ROWS = n_lat // GRID_W
    row = jnp.repeat(jnp.arange(ROWS), GRID_W)
    col = jnp.tile(jnp.arange(GRID_W), ROWS)
    r_axis = MLA_ROPE // 2
    inv_freq = ROPE_THETA ** (-jnp.arange(0, r_axis, 2, dtype=f32) / r_axis)
    ang = jnp.stack([row, col], axis=-1).astype(f32)[:, :, None] * inv_freq
    cos = jnp.cos(ang).astype(x.dtype)
    sin = jnp.sin(ang).astype(x.dtype)
    lb_p = jax.nn.softmax(ev_lb_logits.astype(f32), axis=0)
    lbs = jnp.cumsum(lb_p, axis=0) - lb_p[0]

    silu_c = jax.nn.silu(c)
    silu_cc = jax.nn.silu(c_ctx)
    xl, xc = x, ctx
    for l in range(DEPTH):
        with_ctx = l < DEPTH - 1
        mod_l = (silu_c @ ada_w[l] + ada_b[l])[:, None, :]
        mod_c = (silu_cc @ ada_w[l] + ada_b[l])[None, None, :]
        sh1, sc1, g1, sh2, sc2, g2 = jnp.split(mod_l, 6, axis=-1)
        csh1, csc1, cg1, csh2, csc2, cg2 = jnp.split(mod_c, 6, axis=-1)
        hl = modulate(xl, norm_mix_w[l], sh1, sc1)
        hc = modulate(xc, norm_mix_w[l], csh1, csc1)
        if l % 2 == 0:
            e = l // 2
            yl, yc = even_mixer(hl, hc, ev_w_in[e], lbs[e], ev_onorm_w[e], ev_vnorm_w[e],
                                ev_ws[e], ev_bs[e], ev_w_out[e], with_ctx)
        else:
            o = l // 2
            yl, yc = odd_mixer(hl, hc, od_w_in[o], od_qa_norm_w[o], od_w_qb[o], od_kva_norm_w[o],
                               od_w_kvb[o], od_q_norm_w[o], od_k_norm_w[o], od_w_out[o],
                               cos, sin, with_ctx)
        xl = xl + g1 * yl
        xl = xl + g2 * conv_ffn(modulate(xl, norm_ffn_w[l], sh2, sc2), ffn_w_up[l],
                                ffn_conv_w[l], ffn_conv_b[l], ffn_w_down[l])
        if with_ctx:
            xc = xc + cg1 * yc
            xc = xc + cg2 * conv_ffn(modulate(xc, norm_ffn_w[l], csh2, csc2), ffn_w_up[l],
                                     ffn_conv_w[l], ffn_conv_b[l], ffn_w_down[l])
    return xl
```

```python
import numpy as np
import ml_dtypes
import concourse.bass as bass
import concourse.mybir as mybir
from concourse.bass_utils import run_bass_kernel_spmd

F32 = mybir.dt.float32
BF16 = mybir.dt.bfloat16
AF = mybir.ActivationFunctionType
ALU = mybir.AluOpType
AX = mybir.AxisListType

D = 1024
NLAT = 4096
NCTX = 256
T = NLAT + NCTX
DEPTH = 4
EPS = 1e-6
F_MIN = 1e-6
DFF = 2816
NFC = DFF // 128
EV_IN = 3584
OD_IN = 960
TILES = [(i * 512, 512) for i in range(8)] + [(NLAT, NCTX)]
NT = len(TILES)


class Res:
    __slots__ = ("name", "w", "r")

    def __init__(self, name=""):
        self.name = name
        self.w = None
        self.r = []


class MK:
    ENGS = ("pe", "act", "dve", "pool", "sp")

    def __init__(self, nc):
        self.nc = nc
        self.q = {e: [] for e in self.ENGS}
        self.cnt = {e: 0 for e in self.ENGS}
        self.known = {e: {} for e in self.ENGS}
        self.dcnt = {}
        self.sems = {}

    def _collect(self, eng, reads, writes):
        need = {}

        def add(tok, kind):
            if tok is None:
                return
            k, v = tok
            if k == eng and kind != "raw" and eng == "pe":
                return
            if need.get(k, 0) < v:
                need[k] = v

        for r in reads:
            add(r.w, "raw")
        for w in writes:
            add(w.w, "waw")
            for t in w.r:
                add(t, "war")
        kn = self.known[eng]
        waits = []
        for k, v in need.items():
            if kn.get(k, 0) < v:
                kn[k] = v
                waits.append((k, v))
        return waits

    def _commit(self, tok, reads, writes):
        for r in reads:
            r.r.append(tok)
        for w in writes:
            w.w = tok
            w.r = []

    def op(self, eng, fn, reads=(), writes=(), same_ok=False):
        reads = [r for r in reads if r is not None]
        writes = [w for w in writes if w is not None]
        waits = self._collect(eng, reads, writes)
        if same_ok:
            waits = [(k, v) for (k, v) in waits if k != eng]
        self.cnt[eng] += 1
        tok = (eng, self.cnt[eng])
        self.q[eng].append((fn, waits, (eng, 1)))
        self._commit(tok, reads, writes)
        return tok

    def dma(self, eng, out, in_, reads=(), writes=(), sem=None):
        reads = [r for r in reads if r is not None]
        writes = [w for w in writes if w is not None]
        waits = self._collect(eng, reads, writes)
        self.dcnt[sem] = self.dcnt.get(sem, 0) + 1
        tok = (sem, 16 * self.dcnt[sem])

        def fn(e, out=out, in_=in_):
            return e.dma_start(out=out, in_=in_)

        self.q[eng].append((fn, waits, (sem, 16)))
        self._commit(tok, reads, writes)
        return tok

    def barrier(self):
        toks = [(e, self.cnt[e]) for e in self.ENGS if self.cnt[e] > 0]
        toks += [(k, 16 * c) for k, c in self.dcnt.items()]
        for e in self.ENGS:
            kn = self.known[e]
            waits = []
            for k, v in toks:
                if k == e:
                    continue
                if kn.get(k, 0) < v:
                    kn[k] = v
                    waits.append((k, v))
            if waits:
                self.q[e].append((None, waits, None))

    def emit(self):
        nc = self.nc
        keys = sorted(set(self.ENGS) | set(self.dcnt.keys()))
        assert len(keys) <= 96, len(keys)
        for k in keys:
            self.sems[k] = nc.alloc_semaphore("s_" + k)
        sems = self.sems

        def run(ekey, eng):
            for fn, waits, inc in self.q[ekey]:
                for k, v in waits:
                    eng.wait_ge(sems[k], v)
                if fn is None:
                    continue
                ins = fn(eng)
                if inc is not None:
                    ins.then_inc(sems[inc[0]], inc[1])

        with nc.Block() as block:
            @block.tensor
            def _(e):
                run("pe", e)

            @block.scalar
            def _(e):
                run("act", e)

            @block.vector
            def _(e):
                run("dve", e)

            @block.gpsimd
            def _(e):
                run("pool", e)

            @block.sync
            def _(e):
                run("sp", e)


class Tl:
    def __init__(self, ap, name):
        self.ap = ap
        self.res = Res(name)
        self.key = "d_" + name


class Ring:
    def __init__(self, tiles):
        self.t = tiles
        self.i = 0

    def next(self):
        t = self.t[self.i % len(self.t)]
        self.i += 1
        return t


class Prog:
    def __init__(self, nlayers=DEPTH, stop=None):
        self.nlayers = nlayers
        self.stop = stop
        nc = bass.Bass("TRN2", target_bir_lowering=False)
        self.nc = nc
        self.mk = MK(nc)
        self.din = {}
        self.declare_inputs()
        self.out = nc.dram_tensor("out", [NLAT, D], F32, kind="ExternalOutput").ap()
        self.xs = nc.dram_tensor("xs_scr", [NT, 128, 8, 512], F32).ap()
        self.Ms = nc.dram_tensor("ms_scr", [NT, 128, 8, 512], BF16).ap()
        self.As = nc.dram_tensor("as_scr", [NT, 128, NFC, 512], BF16).ap()
        self.r_xs = [Res("xs%d" % i) for i in range(NT)]
        self.r_Ms = [Res("Ms%d" % i) for i in range(NT)]
        self.r_As = [Res("As%d" % i) for i in range(NT)]
        self.cst = nc.alloc_sbuf_tensor("cst", [128, 6144], BF16).ap()
        self.cst_off = 0
        self.ARENA = 100096
        self.arena = nc.alloc_sbuf_tensor("arena", [128, self.ARENA], BF16).ap()
        self.a_off = 0
        self.ps = [Tl(nc.alloc_psum_tensor("ps%d" % i, [128, 512], F32).ap(), "ps%d" % i) for i in range(8)]
        self.uid = 0
        self.out_toks = []
        self.keymap = {}
        self.build()
        self.mk.emit()

    def _carve(self, base, off, free, dt, parts):
        n = int(np.prod(free))
        sz = n * (2 if dt == F32 else 1)
        a = base[0:parts, off:off + sz]
        if dt == F32:
            a = a.bitcast(F32)
        if len(free) == 2:
            a = a.rearrange("p (a b) -> p a b", a=free[0])
        elif len(free) == 3:
            a = a.rearrange("p (a b c) -> p a b c", a=free[0], b=free[1])
        return a, sz

    def calloc(self, name, free, dt=F32, parts=128):
        free = list(free) if isinstance(free, (list, tuple)) else [free]
        a, sz = self._carve(self.cst, self.cst_off, free, dt, parts)
        self.cst_off += (sz + 15) // 16 * 16
        assert self.cst_off <= 6144, self.cst_off
        return Tl(a, name)

    def alloc(self, name, free, dt=F32, parts=128):
        free = list(free) if isinstance(free, (list, tuple)) else [free]
        a, sz = self._carve(self.arena, self.a_off, free, dt, parts)
        self.a_off += (sz + 15) // 16 * 16
        assert self.a_off <= self.ARENA, (name, self.a_off)
        return Tl(a, name)

    def phase(self):
        self.barrier()
        self.a_off = 0

    def barrier(self):
        self.mk.barrier()
        self.keymap = {}

    def act(self, out, in_, func, reads, writes, scale=None, bias=None, accum=None):
        kw = dict(out=out, in_=in_, func=func)
        if scale is not None:
            kw["scale"] = scale
        if bias is not None:
            kw["bias"] = bias
        if accum is not None:
            kw["accum_out"] = accum
        return self.mk.op("act", lambda e: e.activation(**kw), [t.res for t in reads], [t.res for t in writes])

    def tt(self, eng, out, in0, in1, op, reads, writes):
        return self.mk.op(eng, lambda e: e.tensor_tensor(out=out, in0=in0, in1=in1, op=op),
                          [t.res for t in reads], [t.res for t in writes])

    def ts(self, eng, out, in0, s1, s2, op0, op1, reads, writes):
        return self.mk.op(eng, lambda e: e.tensor_scalar(out=out, in0=in0, scalar1=s1, scalar2=s2, op0=op0, op1=op1),
                          [t.res for t in reads], [t.res for t in writes])

    def stt(self, eng, out, in0, scalar, in1, op0, op1, reads, writes):
        return self.mk.op(eng, lambda e: e.scalar_tensor_tensor(out=out, in0=in0, scalar=scalar, in1=in1, op0=op0, op1=op1),
                          [t.res for t in reads], [t.res for t in writes])

    def cp(self, eng, out, in_, reads, writes):
        if eng == "act":
            return self.act(out, in_, AF.Copy, reads, writes)
        return self.mk.op(eng, lambda e: e.tensor_copy(out=out, in_=in_), [t.res for t in reads], [t.res for t in writes])

    def memset(self, eng, out, val, writes):
        return self.mk.op(eng, lambda e: e.memset(out, val), [], [t.res for t in writes])

    def mm(self, out, lhsT, rhs, start, stop, reads, writes):
        return self.mk.op("pe", lambda e: e.matmul(out, lhsT, rhs, start=start, stop=stop),
                          [t.res for t in reads], [t.res for t in writes])

    def tr(self, out, in_, ident, reads, writes):
        return self.mk.op("pe", lambda e: e.transpose(out, in_, ident), [t.res for t in reads], [t.res for t in writes])

    def dma(self, eng, out, in_, reads, writes, key):
        if key not in self.keymap:
            self.keymap[key] = "k%d" % len(self.keymap)
        key = self.keymap[key]
        return self.mk.dma(eng, out, in_, [r if isinstance(r, Res) else r.res for r in reads],
                           [w if isinstance(w, Res) else w.res for w in writes], sem=key)

    def rstd(self, out, in_ps, n_feat, reads, writes):
        self.act(out, in_ps, AF.Sqrt, reads, writes, scale=1.0 / n_feat, bias=self.eps_t.ap[0:out.shape[0], 0:1])
        self.mk.op("dve", lambda e: e.reciprocal(out=out, in_=out), [t.res for t in writes], [t.res for t in writes])

    def declare_inputs(self):
        nc = self.nc

        def di(name, shape, dt=F32):
            self.din[name] = nc.dram_tensor(name, list(shape), dt, kind="ExternalInput").ap()

        di("x", [NLAT, D]); di("ctx", [NCTX, D]); di("cvec", [128, 8, 2])
        di("ada_w", [DEPTH, D, 6 * D]); di("ada_b", [128, DEPTH, 48])
        di("norm_mix_w", [128, DEPTH, 8]); di("norm_ffn_w", [128, DEPTH, 8])
        di("ev_w_in", [2, D, EV_IN]); di("ev_lb", [128, 2, 2, 4]); di("ev_onorm_w", [128, 2])
        di("ev_vnorm_w", [1, 2 * 512]); di("ev_wsT", [2, 4, 128, 128]); di("ev_bs", [1, 2 * 4 * 128])
        di("ev_w_out", [2, D, D]); di("od_w_in", [2, D, OD_IN]); di("od_qa_w", [128, 2, 3])
        di("od_w_qb", [2, 384, 1152]); di("od_kva_w", [128, 2, 2]); di("od_w_kvb", [2, 256, 1536])
        di("od_qn_nope", [128, 2]); di("od_qn_pe", [64, 2]); di("od_kn_nope", [128, 2]); di("od_kn_pe", [64, 2])
        di("od_w_out", [2, D, D]); di("ffn_w_up", [DEPTH, D, 2 * DFF]); di("ffn_conv_w", [128, DEPTH, 3, NFC])
        di("ffn_conv_b", [128, DEPTH, NFC]); di("ffn_w_down", [DEPTH, DFF, D])
        di("rope_cos", [64, NLAT]); di("rope_sin", [64, NLAT]); di("rope_rot", [64, 64])
        di("dft_c", [NLAT, NLAT], BF16); di("dft_ns", [NLAT, NLAT], BF16)
        di("dftc_c", [NCTX, NCTX], BF16); di("dftc_ns", [NCTX, NCTX], BF16)
        di("dft_cc", [128, 128], BF16); di("dft_ss", [128, 128], BF16)

    def build(self):
        self.setup_consts()
        self.phase_pre()
        self.phase_mods()
        for l in range(self.nlayers):
            self.layer_scalars(l)
            if l % 2 == 0 and self.stop != "odd1":
                self.even_mixer(l)
            else:
                self.odd_mixer(l)
            self.proj_res(l, "mix")
            if self.stop in ("mix", "odd1") and l == self.nlayers - 1:
                break
            self.ffn_up(l)
            self.proj_res(l, "ffn")
        self.phase_post()

    def setup_consts(self):
        d = self.din
        self.eps_t = self.calloc("eps", 2)
        self.memset("pool", self.eps_t.ap, EPS, [self.eps_t])
        self.ones_b = self.calloc("ones_b", 128, BF16)
        self.memset("pool", self.ones_b.ap, 1.0, [self.ones_b])
        self.ident_f = self.calloc("ident_f", 128)
        self.memset("pool", self.ident_f.ap, 0.0, [self.ident_f])
        idf = self.ident_f.ap
        self.mk.op("pool", lambda e: e.affine_select(out=idf, in_=idf, pattern=[[-1, 128]], compare_op=ALU.not_equal,
                                                    fill=1.0, base=0, channel_multiplier=1),
                   [self.ident_f.res], [self.ident_f.res])
        self.ident_b = self.calloc("ident_b", 128, BF16)
        self.cp("dve", self.ident_b.ap, self.ident_f.ap, [self.ident_f], [self.ident_b])
        self.mods = self.calloc("mods", [DEPTH, 48, 2])
        self.nmw = self.calloc("nmw", [DEPTH, 8]); self.nfw = self.calloc("nfw", [DEPTH, 8])
        self.dma("sp", self.nmw.ap, d["norm_mix_w"], [], [self.nmw], "c0")
        self.dma("sp", self.nfw.ap, d["norm_ffn_w"], [], [self.nfw], "c1")
        self.A1 = self.calloc("A1", [8, 2]); self.A2 = self.calloc("A2", [8, 2])
        self.cvw = self.calloc("cvw", [DEPTH, 3, NFC]); self.cvb = self.calloc("cvb", [DEPTH, NFC])
        self.dma("sp", self.cvw.ap, d["ffn_conv_w"], [], [self.cvw], "c2")
        self.dma("sp", self.cvb.ap, d["ffn_conv_b"], [], [self.cvb], "c3")

    def phase_pre(self):
        d = self.din
        self.a_off = 0
        xin = Ring([self.alloc("xin%d" % i, [1024]) for i in range(2)])
        stg = Ring([self.alloc("pstg%d" % i, [8, 128]) for i in range(2)])
        pr = Ring(self.ps[0:4])
        for tb in range(T // 128):
            src = d["x"][tb * 128:(tb + 1) * 128, :] if tb < 32 else d["ctx"][(tb - 32) * 128:(tb - 31) * 128, :]
            xi = xin.next()
            self.dma("sp", xi.ap, src, [], [xi], xi.key)
            st = stg.next()
            for half in range(2):
                p = pr.next()
                for kk in range(4):
                    k = half * 4 + kk
                    self.tr(p.ap[:, kk * 128:(kk + 1) * 128], xi.ap[:, k * 128:(k + 1) * 128], self.ident_f.ap,
                            [xi, self.ident_f], [p])
                self.cp("act" if half == 0 else "dve", st.ap[:, half * 4:(half + 1) * 4, :],
                        p.ap.rearrange("p (a b) -> p a b", a=4), [p], [st])
            ti, o = tb // 4, (tb % 4) * 128
            self.dma("sp", self.xs[ti][:, :, o:o + 128], st.ap, [st], [self.r_xs[ti]], st.key)

    def phase_mods(self):
        d = self.din
        self.phase()
        cv = self.alloc("cv", [8, 2]); cvb = self.alloc("cvb16", [8, 2], BF16)
        adab = self.alloc("adab", [DEPTH, 48])
        self.dma("sp", cv.ap, d["cvec"], [], [cv], cv.key)
        self.dma("sp", adab.ap, d["ada_b"], [], [adab], adab.key)
        self.act(cvb.ap, cv.ap, AF.Silu, [cv], [cvb])
        wr = Ring([self.alloc("adw%d" % i, [8, 512], BF16) for i in range(3)])
        for l in range(self.nlayers):
            p = self.ps[l % 2]
            for jb in range(12):
                w = wr.next()
                self.dma("pool", w.ap, d["ada_w"][l][:, jb * 512:(jb + 1) * 512].rearrange("(k p) c -> p k c", p=128),
                         [], [w], w.key)
                for jj in range(4):
                    j = jb * 4 + jj
                    for k in range(8):
                        self.mm(p.ap[:, 2 * j:2 * j + 2], w.ap[:, k, jj * 128:(jj + 1) * 128], cvb.ap[:, k, :],
                                k == 0, k == 7, [w, cvb], [p])
            self.tt("dve", self.mods.ap[:, l], p.ap[:, 0:96].rearrange("p (a b) -> p a b", b=2),
                    adab.ap[:, l].unsqueeze(2).to_broadcast([128, 48, 2]), ALU.add, [p, adab], [self.mods])

    def layer_scalars(self, l):
        for (A, nw, c0) in ((self.A1, self.nmw, 8), (self.A2, self.nfw, 32)):
            self.stt("dve", A.ap, self.mods.ap[:, l, c0:c0 + 8, :], 1.0,
                     nw.ap[:, l].unsqueeze(2).to_broadcast([128, 8, 2]), ALU.add, ALU.mult, [self.mods, nw], [A])

    def norm_pass(self, l, which, nb=2):
        A = self.A1 if which == 1 else self.A2
        sh0 = 0 if which == 1 else 24
        H = [self.alloc("H%d" % i, [8, n], BF16) for i, (t0, n) in enumerate(TILES)]
        mark = self.a_off
        xr = Ring([self.alloc("nx%d" % i, [8, 512]) for i in range(nb)])
        sq = Ring([self.alloc("nsq%d" % i, [8, 512], BF16) for i in range(nb)])
        rs = Ring([self.alloc("nrs%d" % i, [512]) for i in range(2)])
        pr = Ring(self.ps[6:8])
        for i, (t0, n) in enumerate(TILES):
            j = 1 if i == 8 else 0
            xt = xr.next(); s = sq.next(); r = rs.next(); p = pr.next()
            self.dma("sp", xt.ap[:, :, 0:n], self.xs[i][:, :, 0:n], [self.r_xs[i]], [xt], xt.key)
            self.act(s.ap[:, :, 0:n], xt.ap[:, :, 0:n], AF.Square, [xt], [s])
            for k in range(8):
                self.mm(p.ap[:, 0:n], self.ones_b.ap, s.ap[:, k, 0:n], k == 0, k == 7, [s, self.ones_b], [p])
            self.rstd(r.ap[:, 0:n], p.ap[:, 0:n], D, [p], [r])
            self.tt("dve", xt.ap[:, :, 0:n], xt.ap[:, :, 0:n], r.ap[:, 0:n].unsqueeze(1).to_broadcast([128, 8, n]),
                    ALU.mult, [xt, r], [xt])
            for k in range(8):
                self.act(H[i].ap[:, k, :], xt.ap[:, k, 0:n], AF.Identity, [xt, A, self.mods], [H[i]],
                         scale=A.ap[:, k, j:j + 1], bias=self.mods.ap[:, l, sh0 + k, j:j + 1])
        self.barrier()
        self.a_off = mark
        return H

    def proj_res(self, l, kind):
        d = self.din
        self.phase()
        if kind == "mix":
            KC, src, r_src, g0 = 8, self.Ms, self.r_Ms, 16
            wd = (d["ev_w_out"] if (l % 2 == 0 and self.stop != "odd1") else d["od_w_out"])[l // 2]
        else:
            KC, src, r_src, g0 = NFC, self.As, self.r_As, 40
            wd = d["ffn_w_down"][l]
        W = self.alloc("prW", [KC, D], BF16)
        Wp = []
        for k0 in range(0, KC, 4):
            k1 = min(KC, k0 + 4)
            wp = Tl(W.ap[:, k0:k1, :], "prW%d" % k0)
            self.dma("pool", wp.ap, wd[k0 * 128:k1 * 128, :].rearrange("(k p) c -> p k c", p=128), [], [wp], wp.key)
            Wp.append(wp)
        sr = Ring([self.alloc("prs%d" % i, [KC, 512], BF16) for i in range(2)])
        xr = Ring([self.alloc("prx%d" % i, [8, 512]) for i in range(2)])
        pr = Ring(self.ps[0:6])
        for i, (t0, n) in enumerate(TILES):
            j = 1 if i == 8 else 0
            s = sr.next(); xt = xr.next()
            self.dma("sp", s.ap[:, :, 0:n], src[i][:, :, 0:n], [r_src[i]], [s], s.key)
            self.dma("sp", xt.ap[:, :, 0:n], self.xs[i][:, :, 0:n], [self.r_xs[i]], [xt], xt.key)
            for oc in range(8):
                p = pr.next()
                for k in range(KC):
                    self.mm(p.ap[:, 0:n], W.ap[:, k, oc * 128:(oc + 1) * 128], s.ap[:, k, 0:n], k == 0, k == KC - 1,
                            [s, Wp[k // 4]], [p])
                self.stt("dve", xt.ap[:, oc, 0:n], p.ap[:, 0:n], self.mods.ap[:, l, g0 + oc, j:j + 1],
                         xt.ap[:, oc, 0:n], ALU.mult, ALU.add, [p, xt, self.mods], [xt])
            self.dma("sp", self.xs[i][:, :, 0:n], xt.ap[:, :, 0:n], [xt], [self.r_xs[i]], xt.key + "s")

    def phase_post(self):
        self.phase()
        xr = Ring([self.alloc("pox%d" % i, [8, 512]) for i in range(2)])
        stg = Ring([self.alloc("post%d" % i, [1024]) for i in range(2)])
        pr = Ring(self.ps[0:4])
        for i in range(8):
            xt = xr.next()
            self.dma("sp", xt.ap, self.xs[i], [self.r_xs[i]], [xt], xt.key)
            for b in range(4):
                st = stg.next()
                for half in range(2):
                    p = pr.next()
                    for kk in range(4):
                        k = half * 4 + kk
                        self.tr(p.ap[:, kk * 128:(kk + 1) * 128], xt.ap[:, k, b * 128:(b + 1) * 128], self.ident_f.ap,
                                [xt, self.ident_f], [p])
                    self.cp("act" if half == 0 else "dve", st.ap[:, half * 512:(half + 1) * 512], p.ap, [p], [st])
                r0 = i * 512 + b * 128
                self.out_toks.append(self.dma("sp", self.out[r0:r0 + 128, :], st.ap, [st], [], st.key))
        kn = self.mk.known["sp"]
        waits = []
        for k, v in self.out_toks:
            if kn.get(k, 0) < v:
                kn[k] = v
                waits.append((k, v))
        self.mk.q["sp"].append((None, waits, None))


def ffn_up(self, l):
    d = self.din
    self.phase()
    H = self.norm_pass(l, 2)
    wup = d["ffn_w_up"][l]
    G = Ring([(self.alloc("Gl%d" % i, [NLAT + 2]), self.alloc("Gc%d" % i, [NCTX + 2])) for i in range(2)])
    for gl, gc in G.t:
        for t in (gl, gc):
            self.memset("pool", t.ap, 0.0, [t])
    T1 = Ring([self.alloc("ft%d" % i, [T]) for i in range(2)])
    stg = Ring([self.alloc("fstg%d" % i, [NT, 512], BF16) for i in range(2)])
    wr = Ring([(self.alloc("fwg%d" % i, [8, 256], BF16), self.alloc("fwv%d" % i, [8, 256], BF16)) for i in range(2)])
    pr = Ring(self.ps[0:6])
    segs = ((0, NLAT), (NLAT, NCTX))

    def gate(c, wg, cc, gl, gc):
        for i, (t0, n) in enumerate(TILES):
            p = pr.next()
            for k in range(8):
                self.mm(p.ap[:, 0:n], wg.ap[:, k, cc * 128:(cc + 1) * 128], H[i].ap[:, k, :], k == 0, k == 7, [wg, H[i]], [p])
            dst, off = (gl, 1 + t0) if i < 8 else (gc, 1)
            self.cp("act" if i % 2 == 0 else "dve", dst.ap[:, off:off + n], p.ap[:, 0:n], [p], [dst])

    def conv(c, gl, gc, t1):
        for (g, (s0, sn)) in zip((gl, gc), segs):
            o = t1.ap[:, s0:s0 + sn]
            self.act(o, g.ap[:, 1:1 + sn], AF.Identity, [g, self.cvw, self.cvb], [t1],
                     scale=self.cvw.ap[:, l, 1, c:c + 1], bias=self.cvb.ap[:, l, c:c + 1])
            self.stt("dve", o, g.ap[:, 0:sn], self.cvw.ap[:, l, 0, c:c + 1], o, ALU.mult, ALU.add, [g, t1, self.cvw], [t1])
            self.stt("dve", o, g.ap[:, 2:2 + sn], self.cvw.ap[:, l, 2, c:c + 1], o, ALU.mult, ALU.add, [g, t1, self.cvw], [t1])
            self.act(o, o, AF.Silu, [t1], [t1])

    def val(c, wv, cc, t1):
        st = stg.next()
        for i, (t0, n) in enumerate(TILES):
            p = pr.next()
            for k in range(8):
                self.mm(p.ap[:, 0:n], wv.ap[:, k, cc * 128:(cc + 1) * 128], H[i].ap[:, k, :], k == 0, k == 7, [wv, H[i]], [p])
            self.tt("dve", st.ap[:, i, 0:n], p.ap[:, 0:n], t1.ap[:, t0:t0 + n], ALU.mult, [p, t1], [st])
        self.dma("sp", self.As[:, :, c, :].rearrange("i p t -> p i t"), st.ap, [st], self.r_As, st.key)

    wcur = {}

    def getw(c):
        cb = c // 2
        if cb not in wcur:
            wg, wv = wr.next()
            self.dma("pool", wg.ap, wup[:, cb * 256:(cb + 1) * 256].rearrange("(k p) c -> p k c", p=128), [], [wg], wg.key)
            self.dma("pool", wv.ap, wup[:, DFF + cb * 256:DFF + (cb + 1) * 256].rearrange("(k p) c -> p k c", p=128), [], [wv], wv.key)
            wcur[cb] = (wg, wv)
        return wcur[cb] + (c % 2,)

    gbuf = {}
    wg, wv, cc = getw(0)
    gbuf[0] = G.next()
    gate(0, wg, cc, *gbuf[0])
    for c in range(NFC):
        if c + 1 < NFC:
            wg2, wv2, cc2 = getw(c + 1)
            gbuf[c + 1] = G.next()
            gate(c + 1, wg2, cc2, *gbuf[c + 1])
        t1 = T1.next()
        conv(c, gbuf[c][0], gbuf[c][1], t1)
        wg, wv, cc = getw(c)
        val(c, wv, cc, t1)


Prog.ffn_up = ffn_up


def even_mixer(self, l):
    e = l // 2
    d = self.din
    self.phase()
    H = self.norm_pass(l, 1)
    win = d["ev_w_in"][e]
    CH = 64
    NCH = T // CH
    lb = self.alloc("lb", [2, 4]); oml = self.alloc("oml", [2, 4])
    if e == 0:
        self.memset("pool", lb.ap, 0.0, [lb])
    else:
        lg = self.alloc("lblog", [2, 2, 4])
        self.dma("sp", lg.ap, d["ev_lb"], [], [lg], lg.key)
        self.tt("dve", lb.ap, lg.ap[:, 1], lg.ap[:, 0], ALU.subtract, [lg], [lb])
        self.act(lb.ap, lb.ap, AF.Sigmoid, [lb], [lb])
    self.ts("dve", oml.ap, lb.ap, -1.0, 1.0, ALU.mult, ALU.add, [lb], [oml])
    onw = self.alloc("onw", [2]); self.dma("sp", onw.ap, d["ev_onorm_w"], [], [onw], onw.key)
    mask01 = self.alloc("mask01", [512])
    self.memset("pool", mask01.ap, 1.0, [mask01])
    self.memset("pool", mask01.ap.rearrange("p (c t) -> p c t", t=CH)[:, :, 0:1], 0.0, [mask01])
    masks = []
    for nm, pat, cm in (("mskF", 1, -1), ("mskB", -1, 1)):
        m = self.alloc(nm, [CH], parts=CH)
        self.memset("pool", m.ap, 1.0, [m])
        mp = m.ap
        self.mk.op("pool", lambda en, mp=mp, pat=pat, cm=cm: en.affine_select(out=mp, in_=mp, pattern=[[pat, CH]], compare_op=ALU.is_ge,
                                                                             fill=0.0, base=0, channel_multiplier=cm), [m.res], [m.res])
        masks.append(m)
    hm = []
    for nm, lo, hi in (("hmF", 1.0, 0.0), ("hmB", 0.0, 1.0)):
        m = self.alloc(nm, [512], BF16)
        m3 = m.ap.rearrange("p (c t) -> p c t", t=CH)
        self.memset("pool", m3[:, :, 0:CH // 2], lo, [m])
        self.memset("pool", m3[:, :, CH // 2:CH], hi, [m])
        hm.append(m)
    vnw = self.alloc("vnw", [512]); bsb = self.alloc("bsb", [512])
    self.dma("sp", vnw.ap, d["ev_vnorm_w"][0:1, e * 512:(e + 1) * 512].partition_broadcast(128), [], [vnw], vnw.key)
    self.dma("sp", bsb.ap, d["ev_bs"][0:1, e * 512:(e + 1) * 512].partition_broadcast(128),
             [], [bsb], bsb.key)
    mark_gla = self.a_off
    W5 = self.alloc("W5", [5, 8, 128], BF16)
    Qt = [self.alloc("Qt%d" % i, [T], BF16) for i in range(2)]
    Kt = [self.alloc("Kt%d" % i, [T], BF16) for i in range(2)]
    Qr = [[Res("Qt%d_%d" % (dr, i)) for i in range(NT)] for dr in range(2)]
    Kr = [[Res("Kt%d_%d" % (dr, i)) for i in range(NT)] for dr in range(2)]
    V = self.alloc("V", [NCH, 128], BF16, parts=CH)
    Vr = [Res("V%d" % i) for i in range(NT)]
    O = [self.alloc("O%d" % i, [T], BF16) for i in range(2)]
    Or = [[Res("O%d_%d" % (dr, i)) for i in range(NT)] for dr in range(2)]
    E = [[self.alloc("E%d_%d" % (dr, q), [NCH]) for q in range(3)] for dr in range(2)]
    S = [self.alloc("S%d" % i, [128]) for i in range(2)]
    Sb = [self.alloc("Sb%d" % i, [128], BF16) for i in range(2)]
    KtT = Ring([self.alloc("KtT%d" % i, [8, 128], BF16, parts=CH) for i in range(2)])
    AT = Ring([self.alloc("AT%d" % i, [8, CH], BF16, parts=CH) for i in range(2)])
    KZ = Ring([self.alloc("KZ%d" % i, [512], BF16) for i in range(2)])
    tq = Ring([self.alloc("tq%d" % i, [512]) for i in range(2)])
    tmps = [Ring([self.alloc("tm%d_%d" % (a, i), [512]) for i in range(2 if a < 4 else 1)]) for a in range(6)]
    t8 = Ring([self.alloc("t8_%d" % i, [8]) for i in range(4)])
    stg = Ring([self.alloc("estg%d" % i, [512], BF16) for i in range(2)])
    sqr = Ring([self.alloc("esq%d" % i, [512], BF16) for i in range(2)])
    pr = Ring(self.ps[0:4])
    pa = Ring(self.ps[4:6])
    pd = Ring(self.ps[6:8])

    class R1:
        def __init__(s, res):
            s.res = res

    for hd in range(4):
        for b, c0 in enumerate((hd * 128, 512 + hd * 128, 1024 + hd * 128, 1536 + hd * 128, 2048 + hd * 128)):
            self.dma("pool", W5.ap[:, b], win[:, c0:c0 + 128].rearrange("(k p) c -> p k c", p=128), [], [W5], W5.key)
        for i, (t0, n) in enumerate(TILES):
            nc_ = n // CH
            ch0 = t0 // CH
            pq = pr.next()
            for k in range(8):
                self.mm(pq.ap[:, 0:n], W5.ap[:, 0, k, :], H[i].ap[:, k, :], k == 0, k == 7, [W5, H[i]], [pq])
            qf = tq.next()
            self.act(qf.ap[:, 0:n], pq.ap[:, 0:n], AF.Silu, [pq], [qf])
            for dr in range(2):
                pz = pr.next()
                for k in range(8):
                    self.mm(pz.ap[:, 0:n], W5.ap[:, 2 + dr, k, :], H[i].ap[:, k, :], k == 0, k == 7, [W5, H[i]], [pz])
                f, g, bt, dt, eq, ek = [r.next() for r in tmps]
                fa, ga, ba, da, eqa, eka = [x.ap[:, 0:n] for x in (f, g, bt, dt, eq, ek)]
                self.act(fa, pz.ap[:, 0:n], AF.Sigmoid, [pz], [f])
                self.ts("dve", fa, fa, oml.ap[:, dr, hd:hd + 1], lb.ap[:, dr, hd:hd + 1], ALU.mult, ALU.add, [f, oml, lb], [f])
                self.ts("pool", fa, fa, F_MIN, 1.0, ALU.max, ALU.mult, [f], [f])
                self.act(ga, fa, AF.Ln, [f], [g])
                self.ts("pool", fa, fa, -1.0, 1.0, ALU.mult, ALU.add, [f, g], [f])
                m01 = mask01.ap[:, 0:n]
                self.mk.op("dve", lambda en, ba=ba, m01=m01, ga=ga: en.tensor_tensor_scan(out=ba, data0=m01, data1=ga, initial=0.0,
                                                                                         op0=ALU.mult, op1=ALU.add),
                           [mask01.res, g.res], [bt.res])
                b3 = ba.rearrange("p (c t) -> p c t", t=CH)
                if dr == 0:
                    c3 = b3
                    ctile = bt
                else:
                    self.tt("pool", ga, ga, ba, ALU.subtract, [g, bt], [g])
                    c3 = ga.rearrange("p (c t) -> p c t", t=CH)
                    ctile = g
                tot = b3[:, :, CH - 1:CH]
                cr = c3[:, :, CH // 2 - 1:CH // 2]
                self.tt("dve", da.rearrange("p (c t) -> p c t", t=CH), c3, cr.to_broadcast([128, nc_, CH]), ALU.subtract,
                        [ctile], [dt])
                e1, e2, e3 = [x.ap[:, ch0:ch0 + nc_].unsqueeze(2) for x in E[dr]]
                t8a = t8.next()
                t8v = t8a.ap[:, 0:nc_].unsqueeze(2)
                self.act(e1, tot, AF.Exp, [bt], [E[dr][0]])
                if dr == 0:
                    self.tt("dve", t8v, tot, cr, ALU.subtract, [bt, ctile], [t8a])
                    self.act(e2, t8v, AF.Exp, [t8a], [E[dr][1]])
                    self.act(e3, cr, AF.Exp, [ctile], [E[dr][2]])
                else:
                    self.act(e2, cr, AF.Exp, [ctile], [E[dr][1]], scale=-1.0)
                    self.tt("dve", t8v, tot, cr, ALU.add, [bt, ctile], [t8a])
                    self.act(e3, t8v, AF.Exp, [t8a], [E[dr][2]])
                self.act(eqa, da, AF.Exp, [dt], [eq])
                self.act(eka, da, AF.Exp, [dt], [ek], scale=-1.0)
                self.tt("pool", Qt[dr].ap[:, t0:t0 + n], qf.ap[:, 0:n], eqa, ALU.mult, [qf, eq], [R1(Qr[dr][i])])
                self.tt("dve", Kt[dr].ap[:, t0:t0 + n], fa, eka, ALU.mult, [f, ek], [R1(Kr[dr][i])])
            for h2 in range(0, nc_, 4):
                pv = pr.next()
                for q in range(4):
                    for k in range(8):
                        self.mm(pv.ap[0:CH, q * 128:(q + 1) * 128], H[i].ap[:, k, (h2 + q) * CH:(h2 + q + 1) * CH], W5.ap[:, 1, k, :],
                                k == 0, k == 7, [W5, H[i]], [pv])
                self.cp("act", V.ap[:, ch0 + h2:ch0 + h2 + 4, :], pv.ap[0:CH, :].rearrange("p (a b) -> p a b", a=4),
                        [pv], [R1(Vr[i])])
        for dr in range(2):
            self.memset("pool", S[dr].ap, 0.0, [S[dr]])
            self.memset("pool", Sb[dr].ap, 0.0, [Sb[dr]])
        order = ([8] + list(range(8)), [8] + list(range(7, -1, -1)))
        seqs = []
        for dr in range(2):
            sq_ = []
            for i in order[dr]:
                t0, n = TILES[i]
                qs = list(range(n // CH))
                if dr == 1:
                    qs = qs[::-1]
                sq_ += [(i, q) for q in qs]
            seqs.append(sq_)
        pos = [0, 0]
        for step in range(NT):
            for dr in range(2):
                i = order[dr][step]
                t0, n = TILES[i]
                nc_ = n // CH
                ch0 = t0 // CH
                qr_, kr_, vr_ = R1(Qr[dr][i]), R1(Kr[dr][i]), R1(Vr[i])
                ptr = pa.next()
                ptb = ptr.ap.bitcast(BF16)
                for q in range(nc_):
                    self.tr(ptb[0:CH, q * 128:(q + 1) * 128], Kt[dr].ap[:, t0 + q * CH:t0 + (q + 1) * CH], self.ident_b.ap,
                            [kr_, self.ident_b], [ptr])
                ktt = KtT.next()
                self.cp("act", ktt.ap[:, 0:nc_, :], ptb[0:CH, 0:nc_ * 128].rearrange("p (a b) -> p a b", a=nc_), [ptr], [ktt])
                kz = KZ.next()
                self.tt("pool", kz.ap[:, 0:n], Kt[dr].ap[:, t0:t0 + n], hm[dr].ap[:, 0:n], ALU.mult, [kr_, hm[dr]], [kz])
                psc = pa.next()
                hf = CH // 2
                for q in range(nc_):
                    sl = slice(t0 + q * CH, t0 + (q + 1) * CH)
                    sl0 = slice(t0 + q * CH, t0 + q * CH + hf)
                    sl1 = slice(t0 + q * CH + hf, t0 + (q + 1) * CH)
                    kzc = kz.ap[:, q * CH:(q + 1) * CH]
                    kfc = Kt[dr].ap[:, sl]
                    self.mm(psc.ap[0:CH, q * CH:q * CH + hf], kzc if dr == 0 else kfc, Qt[dr].ap[:, sl0], True, True,
                            [kr_, qr_, kz], [psc])
                    self.mm(psc.ap[0:CH, q * CH + hf:(q + 1) * CH], kfc if dr == 0 else kzc, Qt[dr].ap[:, sl1], True, True,
                            [kr_, qr_, kz], [psc])
                at = AT.next()
                self.tt("dve", at.ap[:, 0:nc_, :], psc.ap[0:CH, 0:nc_ * CH].rearrange("p (a b) -> p a b", a=nc_),
                        masks[dr].ap.unsqueeze(1).to_broadcast([CH, nc_, CH]), ALU.mult, [psc, masks[dr]], [at])
                po = pr.next()
                qs = list(range(nc_))
                if dr == 1:
                    qs = qs[::-1]
                for q in qs:
                    ch = ch0 + q
                    sl = slice(t0 + q * CH, t0 + (q + 1) * CH)
                    self.mm(po.ap[:, q * CH:(q + 1) * CH], V.ap[:, ch, :], at.ap[:, q, :], True, False, [vr_, at], [po])
                    self.mm(po.ap[:, q * CH:(q + 1) * CH], Sb[dr].ap, Qt[dr].ap[:, sl], False, True, [Sb[dr], qr_], [po])
                    pds = pd.next()
                    self.mm(pds.ap[:, 0:128], ktt.ap[:, q, :], V.ap[:, ch, :], True, True, [ktt, vr_], [pds])
                    self.ts("dve", S[dr].ap, S[dr].ap, E[dr][0].ap[:, ch:ch + 1], 0.0, ALU.mult, ALU.add, [S[dr], E[dr][0]], [S[dr]])
                    self.stt("dve", S[dr].ap, pds.ap[:, 0:128], E[dr][1].ap[:, ch:ch + 1], S[dr].ap, ALU.mult, ALU.add,
                             [pds, S[dr], E[dr][1]], [S[dr]])
                    pos[dr] += 1
                    if pos[dr] < len(seqs[dr]):
                        ni, nq = seqs[dr][pos[dr]]
                        nch = TILES[ni][0] // CH + nq
                        self.ts("dve", Sb[dr].ap, S[dr].ap, E[dr][2].ap[:, nch:nch + 1], 0.0, ALU.mult, ALU.add,
                                [S[dr], E[dr][2]], [Sb[dr]])
                self.cp("act", O[dr].ap[:, t0:t0 + n], po.ap[:, 0:n], [po], [R1(Or[dr][i])])
        for i, (t0, n) in enumerate(TILES):
            osum = tmps[0].next(); sg = tmps[1].next(); rs = tmps[2].next()
            self.tt("pool", osum.ap[:, 0:n], O[0].ap[:, t0:t0 + n], O[1].ap[:, t0:t0 + n], ALU.add,
                    [R1(Or[0][i]), R1(Or[1][i])], [osum])
            sq = sqr.next()
            self.act(sq.ap[:, 0:n], osum.ap[:, 0:n], AF.Square, [osum], [sq])
            pn = pa.next()
            self.mm(pn.ap[:, 0:n], self.ones_b.ap, sq.ap[:, 0:n], True, True, [sq, self.ones_b], [pn])
            self.rstd(rs.ap[:, 0:n], pn.ap[:, 0:n], 128, [pn], [rs])
            pg = pr.next()
            for k in range(8):
                self.mm(pg.ap[:, 0:n], W5.ap[:, 4, k, :], H[i].ap[:, k, :], k == 0, k == 7, [W5, H[i]], [pg])
            self.act(sg.ap[:, 0:n], pg.ap[:, 0:n], AF.Silu, [pg], [sg])
            self.tt("dve", osum.ap[:, 0:n], osum.ap[:, 0:n], rs.ap[:, 0:n], ALU.mult, [osum, rs], [osum])
            st = stg.next()
            self.stt("dve", st.ap[:, 0:n], osum.ap[:, 0:n], onw.ap[:, e:e + 1], sg.ap[:, 0:n], ALU.mult, ALU.mult,
                     [osum, onw, sg], [st])
            self.dma("sp", self.Ms[i][:, hd, 0:n], st.ap[:, 0:n], [st], [self.r_Ms[i]], st.key)
    self.barrier()
    self.a_off = mark_gla
    tmps = [Ring([self.alloc("cm%d_%d" % (a, i), [512]) for i in range(2)]) for a in range(4)]
    stg = Ring([self.alloc("cstg%d" % i, [512], BF16) for i in range(2)])
    Wuv = Ring([self.alloc("Wuv%d" % i, [2, 8, 128], BF16) for i in range(2)])
    wsr = Ring([self.alloc("wsT%d" % i, [128], BF16) for i in range(2)])
    vnb = Ring([self.alloc("vnb%d" % i, [4, 128], BF16) for i in range(2)])
    ss4 = Ring([self.alloc("ss4_%d" % i, [4]) for i in range(2)])
    for g in range(4):
        w = Wuv.next(); wsT = wsr.next()
        self.dma("pool", w.ap[:, 0], win[:, 2560 + g * 128:2560 + (g + 1) * 128].rearrange("(k p) c -> p k c", p=128), [], [w], w.key)
        self.dma("pool", w.ap[:, 1], win[:, 3072 + g * 128:3072 + (g + 1) * 128].rearrange("(k p) c -> p k c", p=128), [], [w], w.key)
        self.dma("pool", wsT.ap, d["ev_wsT"][e, g], [], [wsT], wsT.key)
        for i, (t0, n) in enumerate(TILES):
            nb = n // 128
            pu = pr.next()
            for k in range(8):
                self.mm(pu.ap[:, 0:n], w.ap[:, 0, k, :], H[i].ap[:, k, :], k == 0, k == 7, [w, H[i]], [pu])
            U = tmps[0].next()
            self.act(U.ap[:, 0:n], pu.ap[:, 0:n], AF.Gelu_apprx_tanh, [pu], [U])
            pv = pr.next()
            for b in range(nb):
                for k in range(8):
                    self.mm(pv.ap[:, b * 128:(b + 1) * 128], H[i].ap[:, k, b * 128:(b + 1) * 128], w.ap[:, 1, k, :],
                            k == 0, k == 7, [w, H[i]], [pv])
            gv = tmps[1].next(); sqv = tmps[2].next(); s4 = ss4.next()
            gv3 = gv.ap[:, 0:n].rearrange("p (a b) -> p a b", b=128)
            sq3 = sqv.ap[:, 0:n].rearrange("p (a b) -> p a b", b=128)
            self.act(gv.ap[:, 0:n], pv.ap[:, 0:n], AF.Gelu_apprx_tanh, [pv], [gv])
            self.tt("pool", sqv.ap[:, 0:n], gv.ap[:, 0:n], gv.ap[:, 0:n], ALU.mult, [gv], [sqv])
            s4a = s4.ap[:, 0:nb]
            self.mk.op("dve", lambda en, s4a=s4a, sq3=sq3: en.tensor_reduce(out=s4a, in_=sq3, axis=AX.X, op=ALU.add),
                       [sqv.res], [s4.res])
            self.rstd(s4a, s4a, 128, [s4], [s4])
            self.tt("dve", gv3, gv3, s4a.unsqueeze(2).to_broadcast([128, nb, 128]), ALU.mult, [gv, s4], [gv])
            vn = vnb.next()
            self.tt("pool", vn.ap[:, 0:nb, :], gv3, vnw.ap[:, g * 128:(g + 1) * 128].unsqueeze(1).to_broadcast([128, nb, 128]),
                    ALU.mult, [gv, vnw], [vn])
            psv = pr.next()
            for b in range(nb):
                self.mm(psv.ap[:, b * 128:(b + 1) * 128], vn.ap[:, b, :], wsT.ap, True, True, [vn, wsT], [psv])
            tsv = tmps[3].next()
            self.tt("dve", tsv.ap[:, 0:n].rearrange("p (a b) -> p a b", b=128), psv.ap[:, 0:n].rearrange("p (a b) -> p a b", b=128),
                    bsb.ap[:, g * 128:(g + 1) * 128].unsqueeze(1).to_broadcast([128, nb, 128]), ALU.add, [psv, bsb], [tsv])
            st = stg.next()
            self.tt("pool", st.ap[:, 0:n], tsv.ap[:, 0:n], U.ap[:, 0:n], ALU.mult, [tsv, U], [st])
            self.dma("sp", self.Ms[i][:, 4 + g, 0:n], st.ap[:, 0:n], [st], [self.r_Ms[i]], st.key)


Prog.even_mixer = even_mixer


def odd_mixer(self, l):
    o = l // 2
    d = self.din
    self.phase()

    class R1:
        def __init__(s, res):
            s.res = res

    XT = self.alloc("XT", [2, T], BF16)
    QL = self.alloc("QL", [3, T], BF16)
    KVL = self.alloc("KVL", [2, T], BF16)
    KPE = self.alloc("KPE", [T], BF16, parts=64)
    XTr = [Res("XT%d" % i) for i in range(NT)]
    QLr = [Res("QL%d" % i) for i in range(NT)]
    KVLr = [Res("KVL%d" % i) for i in range(NT)]
    KPEr = [Res("KPE%d" % i) for i in range(NT)]
    qaw = self.alloc("qaw", [2, 3]); kvaw = self.alloc("kvaw", [2, 2])
    qnn = self.alloc("qnn", [2]); knn = self.alloc("knn", [2])
    qnp = self.alloc("qnp", [2], parts=64); knp = self.alloc("knp", [2], parts=64)
    rot = self.alloc("rot", [64], BF16, parts=64)
    for t_, nm in ((qaw, "od_qa_w"), (kvaw, "od_kva_w"), (qnn, "od_qn_nope"), (knn, "od_kn_nope"), (qnp, "od_qn_pe"), (knp, "od_kn_pe")):
        self.dma("sp", t_.ap, d[nm], [], [t_], t_.key)
    self.dma("pool", rot.ap, d["rope_rot"], [], [rot], rot.key)
    sqr = Ring([self.alloc("osq%d" % i, [3, 512], BF16) for i in range(2)])
    rsr = Ring([self.alloc("ors%d" % i, [512]) for i in range(2)])
    cosr = Ring([(self.alloc("cos%d" % i, [512], parts=64), self.alloc("sin%d" % i, [512], parts=64)) for i in range(2)])
    rtmp = Ring([(self.alloc("rpf%d" % i, [512], parts=64), self.alloc("rpb%d" % i, [512], BF16, parts=64),
                  self.alloc("rpc%d" % i, [512], parts=64)) for i in range(2)])
    pmark = self.a_off
    pr = Ring(self.ps[0:3])
    pn_ring = Ring([self.ps[3]])

    def rms_group(pss, parts, n, nfeat, wcols, outs, out_tiles):
        sq = sqr.next()
        for c, p in enumerate(pss):
            self.act(sq.ap[0:parts, c, 0:n], p.ap[0:parts, 0:n], AF.Square, [p], [sq])
        pn = pn_ring.next()
        for c in range(len(pss)):
            self.mm(pn.ap[:, 0:n], self.ones_b.ap[0:parts, :], sq.ap[0:parts, c, 0:n], c == 0, c == len(pss) - 1,
                    [sq, self.ones_b], [pn])
        rs = rsr.next()
        self.rstd(rs.ap[:, 0:n], pn.ap[:, 0:n], nfeat, [pn], [rs])
        for c, p in enumerate(pss):
            self.stt("dve", outs[c], p.ap[0:parts, 0:n], wcols[c], rs.ap[0:parts, 0:n], ALU.mult, ALU.mult,
                     [p, rs], [out_tiles[c]])

    def rope(p, wcol, t0, n, out_ap, out_tile):
        pf, pb, pc = rtmp.next()
        rms_group([p], 64, n, 64, [wcol], [pf.ap[:, 0:n]], [pf])
        cs, sn = cosr.next()
        self.dma("sp", cs.ap[:, 0:n], d["rope_cos"][:, t0:t0 + n], [], [cs], cs.key)
        self.dma("sp", sn.ap[:, 0:n], d["rope_sin"][:, t0:t0 + n], [], [sn], sn.key)
        self.cp("act", pb.ap[:, 0:n], pf.ap[:, 0:n], [pf], [pb])
        p2 = pr.next()
        self.mm(p2.ap[0:64, 0:n], rot.ap, pb.ap[:, 0:n], True, True, [rot, pb], [p2])
        self.tt("dve", pc.ap[:, 0:n], p2.ap[0:64, 0:n], sn.ap[:, 0:n], ALU.mult, [p2, sn], [pc])
        self.tt("pool", pf.ap[:, 0:n], pf.ap[:, 0:n], cs.ap[:, 0:n], ALU.mult, [pf, cs], [pf])
        self.tt("pool", out_ap, pf.ap[:, 0:n], pc.ap[:, 0:n], ALU.add, [pf, pc], [out_tile])

    H = self.norm_pass(l, 1, nb=1)
    Win = self.alloc("oWin", [8, OD_IN], BF16)
    wod = d["od_w_in"][o]
    self.dma("pool", Win.ap[:, 0:4], wod[0:512, :].rearrange("(k p) c -> p k c", p=128), [], [Win], Win.key)
    self.dma("pool", Win.ap[:, 4:8], wod[512:1024, :].rearrange("(k p) c -> p k c", p=128), [], [Win], Win.key)

    def proj(c0, m, i, n):
        p = pr.next()
        for k in range(8):
            self.mm(p.ap[0:m, 0:n], Win.ap[:, k, c0:c0 + m], H[i].ap[:, k, :], k == 0, k == 7, [Win, H[i]], [p])
        return p

    for i, (t0, n) in enumerate(TILES):
        for cc in range(2):
            p = proj(cc * 128, 128, i, n)
            self.cp("act" if cc == 0 else "dve", XT.ap[:, cc, t0:t0 + n], p.ap[:, 0:n], [p], [R1(XTr[i])])
        pss = [proj(256 + c * 128, 128, i, n) for c in range(3)]
        rms_group(pss, 128, n, 384, [qaw.ap[:, o, c:c + 1] for c in range(3)],
                  [QL.ap[:, c, t0:t0 + n] for c in range(3)], [R1(QLr[i])] * 3)
        pss = [proj(640 + c * 128, 128, i, n) for c in range(2)]
        rms_group(pss, 128, n, 256, [kvaw.ap[:, o, c:c + 1] for c in range(2)],
                  [KVL.ap[:, c, t0:t0 + n] for c in range(2)], [R1(KVLr[i])] * 2)
        p = proj(896, 64, i, n)
        if i < 8:
            rope(p, knp.ap[:, o:o + 1], t0, n, KPE.ap[:, t0:t0 + n], R1(KPEr[i]))
        else:
            rms_group([p], 64, n, 64, [knp.ap[:, o:o + 1]], [KPE.ap[:, t0:t0 + n]], [R1(KPEr[i])])
    self.barrier()
    self.a_off = pmark
    ccb = self.alloc("ccb", [128], BF16); ssb = self.alloc("ssb", [128], BF16)
    self.dma("sp", ccb.ap, d["dft_cc"], [], [ccb], ccb.key)
    self.dma("sp", ssb.ap, d["dft_ss"], [], [ssb], ssb.key)
    XCS = self.alloc("XCS", [T // 128, 512], BF16)
    XCSr = Res("XCS")
    for tb in range(T // 128):
        p = pr.next()
        for cc in range(2):
            xt_ = XT.ap[:, cc, tb * 128:(tb + 1) * 128]
            self.mm(p.ap[:, cc * 128:(cc + 1) * 128], xt_, ccb.ap, True, True, [R1(XTr[tb // 4]), ccb], [p])
            self.mm(p.ap[:, 256 + cc * 128:256 + (cc + 1) * 128], xt_, ssb.ap, True, True, [R1(XTr[tb // 4]), ssb], [p])
        self.cp("act" if tb % 2 == 0 else "dve", XCS.ap[:, tb, :], p.ap, [p], [R1(XCSr)])
    tabr = Ring([self.alloc("tab%d" % i, [8, 512], BF16) for i in range(4)])
    fstg = Ring([self.alloc("fst%d" % i, [512], BF16) for i in range(2)])
    pf2 = Ring(self.ps[4:8])
    for ft in range(8):
        p0, p1 = pf2.next(), pf2.next()
        step = 0
        for tname, so in (("dft_c", 0), ("dft_ns", 256)):
            for pc_ in range(4):
                tab = tabr.next()
                self.dma("sp", tab.ap, d[tname][pc_ * 1024:(pc_ + 1) * 1024, ft * 512:(ft + 1) * 512].rearrange("(b p) f -> p b f", p=128),
                         [], [tab], tab.key)
                for b in range(8):
                    tb = pc_ * 8 + b
                    for cc, p in ((0, p0), (1, p1)):
                        self.mm(p.ap, XCS.ap[:, tb, so + cc * 128:so + (cc + 1) * 128], tab.ap[:, b, :], step == 0, step == 63,
                                [R1(XCSr), tab], [p])
                    step += 1
        for cc, p in ((0, p0), (1, p1)):
            st = fstg.next()
            self.act(st.ap, p.ap, AF.Copy, [p], [st], scale=1.0 / 512.0)
            self.dma("sp", self.Ms[ft][:, cc, :], st.ap, [st], [self.r_Ms[ft]], st.key)
    ctab = [self.alloc("ctab%d" % i, [2, NCTX], BF16) for i in range(2)]
    for t_, nm in zip(ctab, ("dftc_c", "dftc_ns")):
        self.dma("sp", t_.ap, d[nm].rearrange("(b p) f -> p b f", p=128), [], [t_], t_.key)
    p0, p1 = pf2.next(), pf2.next()
    step = 0
    for t_, so in zip(ctab, (0, 256)):
        for b in range(2):
            for cc, p in ((0, p0), (1, p1)):
                self.mm(p.ap[:, 0:NCTX], XCS.ap[:, 32 + b, so + cc * 128:so + (cc + 1) * 128], t_.ap[:, b, :], step == 0, step == 3,
                        [R1(XCSr), t_], [p])
            step += 1
    for cc, p in ((0, p0), (1, p1)):
        st = fstg.next()
        self.act(st.ap[:, 0:NCTX], p.ap[:, 0:NCTX], AF.Copy, [p], [st], scale=1.0 / 128.0)
        self.dma("sp", self.Ms[8][:, cc, 0:NCTX], st.ap[:, 0:NCTX], [st], [self.r_Ms[8]], st.key)
    self.barrier()
    self.a_off = pmark
    Qn = self.alloc("Qn", [T], BF16); Kn = self.alloc("Kn", [T], BF16)
    Qp = self.alloc("Qp", [T], BF16, parts=64)
    V = self.alloc("aV", [T // 128, 128], BF16)
    Qnr = [Res("Qn%d" % i) for i in range(NT)]; Knr = [Res("Kn%d" % i) for i in range(NT)]
    Qpr = [Res("Qp%d" % i) for i in range(NT)]; Vr = [Res("aV%d" % i) for i in range(NT)]
    wqn = self.alloc("wqn", [3, 128], BF16); wqp = self.alloc("wqp", [3, 64], BF16)
    wkn = self.alloc("wkn", [2, 128], BF16); wv = self.alloc("wv", [2, 128], BF16)
    PT = Ring([self.alloc("PT%d" % i, [512], BF16) for i in range(4)])
    rden = Ring([self.alloc("rden%d" % i, [512]) for i in range(2)])
    astg = Ring([self.alloc("astg%d" % i, [512], BF16) for i in range(2)])
    po_r = Ring(self.ps[4:6]); pd_r = Ring(self.ps[6:8])
    wqb, wkvb = d["od_w_qb"][o], d["od_w_kvb"][o]
    SC = 192.0 ** -0.5
    for h in range(6):
        self.dma("pool", wqn.ap, wqb[:, h * 192:h * 192 + 128].rearrange("(k p) c -> p k c", p=128), [], [wqn], wqn.key)
        self.dma("pool", wqp.ap, wqb[:, h * 192 + 128:(h + 1) * 192].rearrange("(k p) c -> p k c", p=128), [], [wqp], wqp.key)
        self.dma("pool", wkn.ap, wkvb[:, h * 256:h * 256 + 128].rearrange("(k p) c -> p k c", p=128), [], [wkn], wkn.key)
        self.dma("pool", wv.ap, wkvb[:, h * 256 + 128:(h + 1) * 256].rearrange("(k p) c -> p k c", p=128), [], [wv], wv.key)
        for i, (t0, n) in enumerate(TILES):
            p = pr.next()
            for k in range(3):
                self.mm(p.ap[:, 0:n], wqn.ap[:, k, :], QL.ap[:, k, t0:t0 + n], k == 0, k == 2, [wqn, R1(QLr[i])], [p])
            rms_group([p], 128, n, 128, [qnn.ap[:, o:o + 1]], [Qn.ap[:, t0:t0 + n]], [R1(Qnr[i])])
            p = pr.next()
            for k in range(3):
                self.mm(p.ap[0:64, 0:n], wqp.ap[:, k, :], QL.ap[:, k, t0:t0 + n], k == 0, k == 2, [wqp, R1(QLr[i])], [p])
            if i < 8:
                rope(p, qnp.ap[:, o:o + 1], t0, n, Qp.ap[:, t0:t0 + n], R1(Qpr[i]))
            else:
                rms_group([p], 64, n, 64, [qnp.ap[:, o:o + 1]], [Qp.ap[:, t0:t0 + n]], [R1(Qpr[i])])
            p = pr.next()
            for k in range(2):
                self.mm(p.ap[:, 0:n], wkn.ap[:, k, :], KVL.ap[:, k, t0:t0 + n], k == 0, k == 1, [wkn, R1(KVLr[i])], [p])
            rms_group([p], 128, n, 128, [knn.ap[:, o:o + 1]], [Kn.ap[:, t0:t0 + n]], [R1(Knr[i])])
            p = pr.next()
            nb = n // 128
            for b in range(nb):
                for k in range(2):
                    self.mm(p.ap[:, b * 128:(b + 1) * 128], KVL.ap[:, k, t0 + b * 128:t0 + (b + 1) * 128], wv.ap[:, k, :],
                            k == 0, k == 1, [wv, R1(KVLr[i])], [p])
            self.cp("act", V.ap[:, t0 // 128:t0 // 128 + nb, :], p.ap[:, 0:n].rearrange("p (a b) -> p a b", b=128), [p], [R1(Vr[i])])
        for qi, (q0, nq) in enumerate(TILES):
            keys = list(range(T // 128)) if qi < 8 else [32, 33]
            po, pdn = po_r.next(), pd_r.next()
            pts = {}

            def qk(kb):
                ps_ = pr.next()
                ksl = slice(kb * 128, (kb + 1) * 128)
                self.mm(ps_.ap[:, 0:nq], Kn.ap[:, ksl], Qn.ap[:, q0:q0 + nq], True, False,
                        [R1(Knr[kb // 4]), R1(Qnr[qi])], [ps_])
                self.mm(ps_.ap[:, 0:nq], KPE.ap[:, ksl], Qp.ap[:, q0:q0 + nq], False, True,
                        [R1(KPEr[kb // 4]), R1(Qpr[qi])], [ps_])
                pt = PT.next()
                self.act(pt.ap[:, 0:nq], ps_.ap[:, 0:nq], AF.Exp, [ps_], [pt], scale=SC)
                pts[kb] = pt

            def pv(idx):
                kb = keys[idx]
                pt = pts.pop(kb)
                self.mm(po.ap[:, 0:nq], V.ap[:, kb, :], pt.ap[:, 0:nq], idx == 0, idx == len(keys) - 1, [R1(Vr[kb // 4]), pt], [po])
                self.mm(pdn.ap[:, 0:nq], self.ones_b.ap, pt.ap[:, 0:nq], idx == 0, idx == len(keys) - 1, [self.ones_b, pt], [pdn])

            LOOK = 2
            for idx in range(len(keys) + LOOK):
                if idx < len(keys):
                    qk(keys[idx])
                if idx >= LOOK:
                    pv(idx - LOOK)
            rd = rden.next()
            self.mk.op("dve", lambda en, a=rd.ap[:, 0:nq], b=pdn.ap[:, 0:nq]: en.reciprocal(out=a, in_=b), [pdn.res], [rd.res])
            st = astg.next()
            self.tt("dve", st.ap[:, 0:nq], po.ap[:, 0:nq], rd.ap[:, 0:nq], ALU.mult, [po, rd], [st])
            self.dma("sp", self.Ms[qi][:, 2 + h, 0:nq], st.ap[:, 0:nq], [st], [self.r_Ms[qi]], st.key)


Prog.odd_mixer = odd_mixer


_PROG = {}


def _consts():
    f32 = np.float32
    bf = ml_dtypes.bfloat16
    rows = NLAT // 64
    row = np.repeat(np.arange(rows), 64)
    col = np.tile(np.arange(64), rows)
    inv_freq = (10000.0 ** (-np.arange(0, 32, 2, dtype=f32) / f32(32))).astype(f32)
    ang = np.stack([row, col], axis=-1).astype(f32)[:, :, None] * inv_freq
    cos = np.cos(ang).astype(f32)
    sin = np.sin(ang).astype(f32)
    cos_f = np.repeat(cos[:, :, None, :], 2, axis=2).reshape(NLAT, 64).T
    sin_f = np.repeat(sin[:, :, None, :], 2, axis=2).reshape(NLAT, 64).T
    R = np.zeros((64, 64), f32)
    for a in range(2):
        for i in range(16):
            d0 = a * 32 + i
            d1 = a * 32 + 16 + i
            R[d1, d0] = -1.0
            R[d0, d1] = 1.0

    def dft(n):
        t = np.arange(n, dtype=np.int64)
        m = (t[:, None] * t[None, :]) % n
        a = 2.0 * np.pi * m.astype(np.float64) / n
        return np.cos(a).astype(bf), (-np.sin(a)).astype(bf)

    c4, ns4 = dft(NLAT)
    cc_, ncs_ = dft(NCTX)
    t = np.arange(64)
    a = 2.0 * np.pi * ((t[:, None] * t[None, :]) % 64) / 64.0
    cc = np.zeros((128, 128), np.float64); ss = np.zeros((128, 128), np.float64)
    for g in range(2):
        cc[g * 64:(g + 1) * 64, g * 64:(g + 1) * 64] = np.cos(a)
        ss[g * 64:(g + 1) * 64, g * 64:(g + 1) * 64] = np.sin(a)
    return dict(rope_cos=np.ascontiguousarray(cos_f), rope_sin=np.ascontiguousarray(sin_f), rope_rot=R,
                dft_c=c4, dft_ns=ns4, dftc_c=cc_, dftc_ns=ncs_, dft_cc=cc.astype(bf), dft_ss=ss.astype(bf))


def _layout(inp):
    f = lambda a: np.ascontiguousarray(np.asarray(a, dtype=np.float32))
    sh = {}
    sh["ada_w"] = f(inp["ada_w"])
    sh["ada_b"] = f(np.asarray(inp["ada_b"]).reshape(DEPTH, 48, 128).transpose(2, 0, 1))
    sh["norm_mix_w"] = f(np.asarray(inp["norm_mix_w"]).reshape(DEPTH, 8, 128).transpose(2, 0, 1))
    sh["norm_ffn_w"] = f(np.asarray(inp["norm_ffn_w"]).reshape(DEPTH, 8, 128).transpose(2, 0, 1))
    sh["ev_w_in"] = f(inp["ev_w_in"])
    sh["ev_lb"] = f(np.asarray(inp["ev_lb_logits"]).reshape(2, 2, 4, 128).transpose(3, 0, 1, 2))
    sh["ev_onorm_w"] = f(np.asarray(inp["ev_onorm_w"]).T)
    sh["ev_vnorm_w"] = f(np.asarray(inp["ev_vnorm_w"]).reshape(1, -1))
    sh["ev_wsT"] = f(np.asarray(inp["ev_ws"]).transpose(0, 1, 3, 2))
    sh["ev_bs"] = f(np.asarray(inp["ev_bs"]).reshape(1, -1))
    sh["ev_w_out"] = f(inp["ev_w_out"])
    sh["od_w_in"] = f(inp["od_w_in"])
    sh["od_qa_w"] = f(np.asarray(inp["od_qa_norm_w"]).reshape(2, 3, 128).transpose(2, 0, 1))
    sh["od_w_qb"] = f(inp["od_w_qb"])
    sh["od_kva_w"] = f(np.asarray(inp["od_kva_norm_w"]).reshape(2, 2, 128).transpose(2, 0, 1))
    sh["od_w_kvb"] = f(inp["od_w_kvb"])
    qn = np.asarray(inp["od_q_norm_w"]); kn = np.asarray(inp["od_k_norm_w"])
    sh["od_qn_nope"] = f(qn[:, :128].T); sh["od_qn_pe"] = f(qn[:, 128:].T)
    sh["od_kn_nope"] = f(kn[:, :128].T); sh["od_kn_pe"] = f(kn[:, 128:].T)
    sh["od_w_out"] = f(inp["od_w_out"])
    sh["ffn_w_up"] = f(inp["ffn_w_up"])
    sh["ffn_conv_w"] = f(np.asarray(inp["ffn_conv_w"]).reshape(DEPTH, 3, NFC, 128).transpose(3, 0, 1, 2))
    sh["ffn_conv_b"] = f(np.asarray(inp["ffn_conv_b"]).reshape(DEPTH, NFC, 128).transpose(2, 0, 1))
    sh["ffn_w_down"] = f(inp["ffn_w_down"])
    sh.update(_consts())
    return sh


def kernel(**inputs):
    nl = int(inputs.pop("_nlayers", DEPTH))
    stop = inputs.pop("_stop", None)
    if (nl, stop) not in _PROG:
        _PROG[(nl, stop)] = Prog(nl, stop)
    prog = _PROG[(nl, stop)]
    sh = _layout(inputs)
    x = np.asarray(inputs["x"], dtype=np.float32)
    c = np.asarray(inputs["c"], dtype=np.float32)
    ctx = np.asarray(inputs["ctx"], dtype=np.float32)
    cc = np.asarray(inputs["c_ctx"], dtype=np.float32)
    B = x.shape[0]
    in_maps = []
    for b in range(B):
        m = dict(sh)
        m["x"] = np.ascontiguousarray(x[b])
        m["ctx"] = np.ascontiguousarray(ctx[b])
        m["cvec"] = np.ascontiguousarray(np.stack([c[b].reshape(8, 128).T, cc.reshape(8, 128).T], axis=-1))
        in_maps.append(m)
    res = run_bass_kernel_spmd(prog.nc, in_maps, core_ids=list(range(B)))
    return np.stack([np.asarray(r["out"], dtype=np.float32) for r in res.results], axis=0)
```

```python
import numpy as np
import ml_dtypes
import concourse.bass as bass
import concourse.mybir as mybir
from concourse.bass_utils import run_bass_kernel_spmd

F32 = mybir.dt.float32
BF16 = mybir.dt.bfloat16
AF = mybir.ActivationFunctionType
ALU = mybir.AluOpType
AX = mybir.AxisListType

D = 1024
NLAT = 4096
NCTX = 256
T = NLAT + NCTX
DEPTH = 4
EPS = 1e-6
F_MIN = 1e-6
DFF = 2816
NFC = DFF // 128
EV_IN = 3584
OD_IN = 960
TILES = [(i * 512, 512) for i in range(8)] + [(NLAT, NCTX)]
NT = len(TILES)


class Res:
    __slots__ = ("name", "w", "r")

    def __init__(self, name=""):
        self.name = name
        self.w = None
        self.r = []


class MK:
    ENGS = ("pe", "act", "dve", "pool", "sp")

    def __init__(self, nc):
        self.nc = nc
        self.q = {e: [] for e in self.ENGS}
        self.cnt = {e: 0 for e in self.ENGS}
        self.known = {e: {} for e in self.ENGS}
        self.dcnt = {}
        self.sems = {}

    def _collect(self, eng, reads, writes):
        need = {}

        def add(tok, kind):
            if tok is None:
                return
            k, v = tok
            if k == eng and kind != "raw" and eng == "pe":
                return
            if need.get(k, 0) < v:
                need[k] = v

        for r in reads:
            add(r.w, "raw")
        for w in writes:
            add(w.w, "waw")
            for t in w.r:
                add(t, "war")
        kn = self.known[eng]
        waits = []
        for k, v in need.items():
            if kn.get(k, 0) < v:
                kn[k] = v
                waits.append((k, v))
        return waits

    def _commit(self, tok, reads, writes):
        for r in reads:
            r.r.append(tok)
        for w in writes:
            w.w = tok
            w.r = []

    def op(self, eng, fn, reads=(), writes=(), same_ok=False):
        reads = [r for r in reads if r is not None]
        writes = [w for w in writes if w is not None]
        waits = self._collect(eng, reads, writes)
        if same_ok:
            waits = [(k, v) for (k, v) in waits if k != eng]
        self.cnt[eng] += 1
        tok = (eng, self.cnt[eng])
        self.q[eng].append((fn, waits, (eng, 1)))
        self._commit(tok, reads, writes)
        return tok

    def dma(self, eng, out, in_, reads=(), writes=(), sem=None):
        reads = [r for r in reads if r is not None]
        writes = [w for w in writes if w is not None]
        waits = self._collect(eng, reads, writes)
        self.dcnt[sem] = self.dcnt.get(sem, 0) + 1
        tok = (sem, 16 * self.dcnt[sem])

        def fn(e, out=out, in_=in_):
            return e.dma_start(out=out, in_=in_)

        self.q[eng].append((fn, waits, (sem, 16)))
        self._commit(tok, reads, writes)
        return tok

    def barrier(self):
        toks = [(e, self.cnt[e]) for e in self.ENGS if self.cnt[e] > 0]
        toks += [(k, 16 * c) for k, c in self.dcnt.items()]
        for e in self.ENGS:
            kn = self.known[e]
            waits = []
            for k, v in toks:
                if k == e:
                    continue
                if kn.get(k, 0) < v:
                    kn[k] = v
                    waits.append((k, v))
            if waits:
                self.q[e].append((None, waits, None))

    def emit(self):
        nc = self.nc
        keys = sorted(set(self.ENGS) | set(self.dcnt.keys()))
        assert len(keys) <= 96, len(keys)
        for k in keys:
            self.sems[k] = nc.alloc_semaphore("s_" + k)
        sems = self.sems

        def run(ekey, eng):
            for fn, waits, inc in self.q[ekey]:
                for k, v in waits:
                    eng.wait_ge(sems[k], v)
                if fn is None:
                    continue
                ins = fn(eng)
                if inc is not None:
                    ins.then_inc(sems[inc[0]], inc[1])

        with nc.Block() as block:
            @block.tensor
            def _(e):
                run("pe", e)

            @block.scalar
            def _(e):
                run("act", e)

            @block.vector
            def _(e):
                run("dve", e)

            @block.gpsimd
            def _(e):
                run("pool", e)

            @block.sync
            def _(e):
                run("sp", e)


class Tl:
    def __init__(self, ap, name):
        self.ap = ap
        self.res = Res(name)
        self.key = "d_" + name


class Ring:
    def __init__(self, tiles):
        self.t = tiles
        self.i = 0

    def next(self):
        t = self.t[self.i % len(self.t)]
        self.i += 1
        return t


class Prog:
    def __init__(self, nlayers=DEPTH, stop=None):
        self.nlayers = nlayers
        self.stop = stop
        nc = bass.Bass("TRN2", target_bir_lowering=False)
        self.nc = nc
        self.mk = MK(nc)
        self.din = {}
        self.declare_inputs()
        self.out = nc.dram_tensor("out", [NLAT, D], F32, kind="ExternalOutput").ap()
        self.xs = nc.dram_tensor("xs_scr", [NT, 128, 8, 512], F32).ap()
        self.Ms = nc.dram_tensor("ms_scr", [NT, 128, 8, 512], BF16).ap()
        self.As = nc.dram_tensor("as_scr", [NT, 128, NFC, 512], BF16).ap()
        self.r_xs = [Res("xs%d" % i) for i in range(NT)]
        self.r_Ms = [Res("Ms%d" % i) for i in range(NT)]
        self.r_As = [Res("As%d" % i) for i in range(NT)]
        self.cst = nc.alloc_sbuf_tensor("cst", [128, 6144], BF16).ap()
        self.cst_off = 0
        self.ARENA = 100096
        self.arena = nc.alloc_sbuf_tensor("arena", [128, self.ARENA], BF16).ap()
        self.a_off = 0
        self.a_limit = self.ARENA
        self.h_fused = None
        self.ps = [Tl(nc.alloc_psum_tensor("ps%d" % i, [128, 512], F32).ap(), "ps%d" % i) for i in range(8)]
        self.uid = 0
        self.out_toks = []
        self.keymap = {}
        self.build()
        self.mk.emit()

    def _carve(self, base, off, free, dt, parts):
        n = int(np.prod(free))
        sz = n * (2 if dt == F32 else 1)
        a = base[0:parts, off:off + sz]
        if dt == F32:
            a = a.bitcast(F32)
        if len(free) == 2:
            a = a.rearrange("p (a b) -> p a b", a=free[0])
        elif len(free) == 3:
            a = a.rearrange("p (a b c) -> p a b c", a=free[0], b=free[1])
        return a, sz

    def calloc(self, name, free, dt=F32, parts=128):
        free = list(free) if isinstance(free, (list, tuple)) else [free]
        a, sz = self._carve(self.cst, self.cst_off, free, dt, parts)
        self.cst_off += (sz + 15) // 16 * 16
        assert self.cst_off <= 6144, self.cst_off
        return Tl(a, name)

    def alloc(self, name, free, dt=F32, parts=128):
        free = list(free) if isinstance(free, (list, tuple)) else [free]
        a, sz = self._carve(self.arena, self.a_off, free, dt, parts)
        self.a_off += (sz + 15) // 16 * 16
        assert self.a_off <= self.a_limit, (name, self.a_off, self.a_limit)
        return Tl(a, name)

    HSZ = 8 * T

    def alloc_H(self):
        base = self.ARENA - self.HSZ
        self.a_limit = base
        assert self.a_off <= base, self.a_off
        H = []
        for i, (t0, n) in enumerate(TILES):
            a = self.arena[:, base + 8 * t0:base + 8 * (t0 + n)].rearrange("p (a b) -> p a b", a=8)
            H.append(Tl(a, "H%d" % i))
        return H

    def free_H(self):
        self.a_limit = self.ARENA

    def phase(self):
        self.barrier()
        self.a_off = 0
        self.a_limit = self.ARENA

    def barrier(self):
        self.mk.barrier()
        self.keymap = {}

    def act(self, out, in_, func, reads, writes, scale=None, bias=None, accum=None):
        kw = dict(out=out, in_=in_, func=func)
        if scale is not None:
            kw["scale"] = scale
        if bias is not None:
            kw["bias"] = bias
        if accum is not None:
            kw["accum_out"] = accum
        return self.mk.op("act", lambda e: e.activation(**kw), [t.res for t in reads], [t.res for t in writes])

    def tt(self, eng, out, in0, in1, op, reads, writes):
        return self.mk.op(eng, lambda e: e.tensor_tensor(out=out, in0=in0, in1=in1, op=op),
                          [t.res for t in reads], [t.res for t in writes])

    def ts(self, eng, out, in0, s1, s2, op0, op1, reads, writes):
        return self.mk.op(eng, lambda e: e.tensor_scalar(out=out, in0=in0, scalar1=s1, scalar2=s2, op0=op0, op1=op1),
                          [t.res for t in reads], [t.res for t in writes])

    def stt(self, eng, out, in0, scalar, in1, op0, op1, reads, writes):
        return self.mk.op(eng, lambda e: e.scalar_tensor_tensor(out=out, in0=in0, scalar=scalar, in1=in1, op0=op0, op1=op1),
                          [t.res for t in reads], [t.res for t in writes])

    def cp(self, eng, out, in_, reads, writes):
        if eng == "act":
            return self.act(out, in_, AF.Copy, reads, writes)
        return self.mk.op(eng, lambda e: e.tensor_copy(out=out, in_=in_), [t.res for t in reads], [t.res for t in writes])

    def memset(self, eng, out, val, writes):
        return self.mk.op(eng, lambda e: e.memset(out, val), [], [t.res for t in writes])

    def mm(self, out, lhsT, rhs, start, stop, reads, writes):
        return self.mk.op("pe", lambda e: e.matmul(out, lhsT, rhs, start=start, stop=stop),
                          [t.res for t in reads], [t.res for t in writes])

    def tr(self, out, in_, ident, reads, writes):
        return self.mk.op("pe", lambda e: e.transpose(out, in_, ident), [t.res for t in reads], [t.res for t in writes])

    def dma(self, eng, out, in_, reads, writes, key):
        km = self.keymap.setdefault(eng == "pool", {})
        if key not in km:
            km[key] = ("g%d" if eng == "pool" else "k%d") % len(km)
        key = km[key]
        return self.mk.dma(eng, out, in_, [r if isinstance(r, Res) else r.res for r in reads],
                           [w if isinstance(w, Res) else w.res for w in writes], sem=key)

    def rstd(self, out, in_ps, n_feat, reads, writes):
        self.act(out, in_ps, AF.Sqrt, list(reads) + [self.eps_t], writes, scale=1.0 / n_feat, bias=self.eps_t.ap[0:out.shape[0], 0:1])
        self.mk.op("dve", lambda e: e.reciprocal(out=out, in_=out), [t.res for t in writes], [t.res for t in writes])

    def declare_inputs(self):
        nc = self.nc

        def di(name, shape, dt=F32):
            self.din[name] = nc.dram_tensor(name, list(shape), dt, kind="ExternalInput").ap()

        di("x", [NLAT, D]); di("ctx", [NCTX, D]); di("cvec", [128, 8, 2])
        di("ada_w", [DEPTH, D, 6 * D]); di("ada_b", [128, DEPTH, 48])
        di("norm_mix_w", [128, DEPTH, 8]); di("norm_ffn_w", [128, DEPTH, 8])
        di("ev_w_in", [2, D, EV_IN]); di("ev_lb", [128, 2, 2, 4]); di("ev_onorm_w", [128, 2])
        di("ev_vnorm_w", [1, 2 * 512]); di("ev_wsT", [2, 4, 128, 128]); di("ev_bs", [1, 2 * 4 * 128])
        di("ev_w_out", [2, D, D]); di("od_w_in", [2, D, OD_IN]); di("od_qa_w", [128, 2, 3])
        di("od_w_qb", [2, 384, 1152]); di("od_kva_w", [128, 2, 2]); di("od_w_kvb", [2, 256, 1536])
        di("od_qn_nope", [128, 2]); di("od_qn_pe", [64, 2]); di("od_kn_nope", [128, 2]); di("od_kn_pe", [64, 2])
        di("od_w_out", [2, D, D]); di("ffn_w_up", [DEPTH, D, 2 * DFF]); di("ffn_conv_w", [128, DEPTH, 3, NFC])
        di("ffn_conv_b", [128, DEPTH, NFC]); di("ffn_w_down", [DEPTH, DFF, D])
        di("rope_cos", [64, NLAT]); di("rope_sin", [64, NLAT]); di("rope_rot", [64, 64])
        di("dft_c", [NLAT, NLAT], BF16); di("dft_ns", [NLAT, NLAT], BF16)
        di("dftc_c", [NCTX, NCTX], BF16); di("dftc_ns", [NCTX, NCTX], BF16)
        di("dft_cc", [128, 128], BF16); di("dft_ss", [128, 128], BF16)

    def build(self):
        self.setup_consts()
        self.phase_pre()
        self.phase_mods()
        self.layer_scalars(0, 1)
        for l in range(self.nlayers):
            self.layer_scalars(l, 2)
            odd = (l % 2 == 1) or self.stop == "odd1"
            if not odd:
                self.even_mixer(l)
            else:
                self.odd_mixer(l)
            last_mix = self.stop in ("mix", "odd1") and l == self.nlayers - 1
            self.proj_res(l, "mix", None if last_mix else (l, 2))
            if last_mix:
                break
            self.ffn_up(l)
            if l + 1 < self.nlayers:
                self.layer_scalars(l + 1, 1)
                self.proj_res(l, "ffn", (l + 1, 1))
            else:
                self.proj_res(l, "ffn", None)
        self.phase_post()

    def setup_consts(self):
        d = self.din
        self.eps_t = self.calloc("eps", 2)
        self.memset("pool", self.eps_t.ap, EPS, [self.eps_t])
        self.ones_b = self.calloc("ones_b", 128, BF16)
        self.memset("pool", self.ones_b.ap, 1.0, [self.ones_b])
        self.ident_f = self.calloc("ident_f", 128)
        self.memset("pool", self.ident_f.ap, 0.0, [self.ident_f])
        idf = self.ident_f.ap
        self.mk.op("pool", lambda e: e.affine_select(out=idf, in_=idf, pattern=[[-1, 128]], compare_op=ALU.not_equal,
                                                    fill=1.0, base=0, channel_multiplier=1),
                   [self.ident_f.res], [self.ident_f.res])
        self.ident_b = self.calloc("ident_b", 128, BF16)
        self.cp("dve", self.ident_b.ap, self.ident_f.ap, [self.ident_f], [self.ident_b])
        self.mods = self.calloc("mods", [DEPTH, 48, 2])
        self.nmw = self.calloc("nmw", [DEPTH, 8]); self.nfw = self.calloc("nfw", [DEPTH, 8])
        self.dma("sp", self.nmw.ap, d["norm_mix_w"], [], [self.nmw], "c0")
        self.dma("sp", self.nfw.ap, d["norm_ffn_w"], [], [self.nfw], "c1")
        self.A1 = self.calloc("A1", [8, 2]); self.A2 = self.calloc("A2", [8, 2])
        self.cvw = self.calloc("cvw", [DEPTH, 3, NFC]); self.cvb = self.calloc("cvb", [DEPTH, NFC])
        self.dma("sp", self.cvw.ap, d["ffn_conv_w"], [], [self.cvw], "c2")
        self.dma("sp", self.cvb.ap, d["ffn_conv_b"], [], [self.cvb], "c3")

    def phase_pre(self):
        d = self.din
        self.a_off = 0
        xin = Ring([self.alloc("xin%d" % i, [1024]) for i in range(2)])
        stg = Ring([self.alloc("pstg%d" % i, [8, 128]) for i in range(2)])
        pr = Ring(self.ps[0:4])
        for tb in range(T // 128):
            src = d["x"][tb * 128:(tb + 1) * 128, :] if tb < 32 else d["ctx"][(tb - 32) * 128:(tb - 31) * 128, :]
            xi = xin.next()
            self.dma("sp", xi.ap, src, [], [xi], xi.key)
            st = stg.next()
            for half in range(2):
                p = pr.next()
                for kk in range(4):
                    k = half * 4 + kk
                    self.tr(p.ap[:, kk * 128:(kk + 1) * 128], xi.ap[:, k * 128:(k + 1) * 128], self.ident_f.ap,
                            [xi, self.ident_f], [p])
                self.cp("act" if half == 0 else "dve", st.ap[:, half * 4:(half + 1) * 4, :],
                        p.ap.rearrange("p (a b) -> p a b", a=4), [p], [st])
            ti, o = tb // 4, (tb % 4) * 128
            self.dma("sp", self.xs[ti][:, :, o:o + 128], st.ap, [st], [], st.key)

    def phase_mods(self):
        d = self.din
        cv = self.alloc("cv", [8, 2]); cvb = self.alloc("cvb16", [8, 2], BF16)
        adab = self.alloc("adab", [DEPTH, 48])
        self.dma("sp", cv.ap, d["cvec"], [], [cv], cv.key)
        self.dma("sp", adab.ap, d["ada_b"], [], [adab], adab.key)
        self.act(cvb.ap, cv.ap, AF.Silu, [cv], [cvb])
        wr = Ring([self.alloc("adw%d" % i, [8, 512], BF16) for i in range(3)])
        for l in range(self.nlayers):
            p = self.ps[l % 2]
            for jb in range(12):
                w = wr.next()
                self.dma("pool", w.ap, d["ada_w"][l][:, jb * 512:(jb + 1) * 512].rearrange("(k p) c -> p k c", p=128),
                         [], [w], w.key)
                for jj in range(4):
                    j = jb * 4 + jj
                    for k in range(8):
                        self.mm(p.ap[:, 2 * j:2 * j + 2], w.ap[:, k, jj * 128:(jj + 1) * 128], cvb.ap[:, k, :],
                                k == 0, k == 7, [w, cvb], [p])
            self.tt("dve", self.mods.ap[:, l], p.ap[:, 0:96].rearrange("p (a b) -> p a b", b=2),
                    adab.ap[:, l].unsqueeze(2).to_broadcast([128, 48, 2]), ALU.add, [p, adab], [self.mods])

    def layer_scalars(self, l, which):
        A, nw, c0 = (self.A1, self.nmw, 8) if which == 1 else (self.A2, self.nfw, 32)
        self.stt("dve", A.ap, self.mods.ap[:, l, c0:c0 + 8, :], 1.0,
                 nw.ap[:, l].unsqueeze(2).to_broadcast([128, 8, 2]), ALU.add, ALU.mult, [self.mods, nw], [A])

    def norm_tile(self, l, which, i, n, xt, H, sq, rs, pr):
        A = self.A1 if which == 1 else self.A2
        sh0 = 0 if which == 1 else 24
        j = 1 if i == 8 else 0
        s = sq.next(); r = rs.next(); p = pr.next()
        self.act(s.ap[:, :, 0:n], xt.ap[:, :, 0:n], AF.Square, [xt], [s])
        for k in range(8):
            self.mm(p.ap[:, 0:n], self.ones_b.ap, s.ap[:, k, 0:n], k == 0, k == 7, [s, self.ones_b], [p])
        self.rstd(r.ap[:, 0:n], p.ap[:, 0:n], D, [p], [r])
        self.tt("dve", H[i].ap, xt.ap[:, :, 0:n], r.ap[:, 0:n].unsqueeze(1).to_broadcast([128, 8, n]),
                ALU.mult, [xt, r], [H[i]])
        for k in range(8):
            self.act(H[i].ap[:, k, :], H[i].ap[:, k, :], AF.Identity, [H[i], A, self.mods], [H[i]],
                     scale=A.ap[:, k, j:j + 1], bias=self.mods.ap[:, l, sh0 + k, j:j + 1])

    def norm_pass(self, l, which, nb=2):
        H = self.alloc_H()
        if self.h_fused == (l, which):
            self.h_fused = None
            return H
        mark = self.a_off
        xr = Ring([self.alloc("nx%d" % i, [8, 512]) for i in range(nb)])
        sq = Ring([self.alloc("nsq%d" % i, [8, 512], BF16) for i in range(nb)])
        rs = Ring([self.alloc("nrs%d" % i, [512]) for i in range(2)])
        pr = Ring(self.ps[6:8])
        for i, (t0, n) in enumerate(TILES):
            xt = xr.next()
            self.dma("sp", xt.ap[:, :, 0:n], self.xs[i][:, :, 0:n], [self.r_xs[i]], [xt], xt.key)
            self.norm_tile(l, which, i, n, xt, H, sq, rs, pr)
        self.barrier()
        self.a_off = mark
        return H

    def proj_res(self, l, kind, nxt=None):
        d = self.din
        self.phase()
        if nxt is not None:
            H = self.alloc_H()
            nsq = Ring([self.alloc("fsq0", [8, 512], BF16)])
            nrs = Ring([self.alloc("frs%d" % i, [512]) for i in range(2)])
            npr = Ring(self.ps[6:8])
        if kind == "mix":
            KC, src, r_src, g0 = 8, self.Ms, self.r_Ms, 16
            wd = (d["ev_w_out"] if (l % 2 == 0 and self.stop != "odd1") else d["od_w_out"])[l // 2]
        else:
            KC, src, r_src, g0 = NFC, self.As, self.r_As, 40
            wd = d["ffn_w_down"][l]
        W = self.alloc("prW", [KC, D], BF16)
        Wp = []
        for k0 in range(0, KC, 4):
            k1 = min(KC, k0 + 4)
            wp = Tl(W.ap[:, k0:k1, :], "prW%d" % k0)
            self.dma("pool", wp.ap, wd[k0 * 128:k1 * 128, :].rearrange("(k p) c -> p k c", p=128), [], [wp], wp.key)
            Wp.append(wp)
        sr = Ring([self.alloc("prs%d" % i, [KC, 512], BF16) for i in range(2)])
        xr = Ring([self.alloc("prx%d" % i, [8, 512]) for i in range(1 if (nxt is not None and kind == "ffn") else 2)])
        pr = Ring(self.ps[0:6])
        pending = None
        for i, (t0, n) in enumerate(TILES):
            j = 1 if i == 8 else 0
            s = sr.next(); xt = xr.next()
            self.dma("sp", s.ap[:, :, 0:n], src[i][:, :, 0:n], [r_src[i]], [s], s.key)
            self.dma("sp", xt.ap[:, :, 0:n], self.xs[i][:, :, 0:n], [self.r_xs[i]], [xt], xt.key)
            for oc in range(8):
                p = pr.next()
                for k in range(KC):
                    self.mm(p.ap[:, 0:n], W.ap[:, k, oc * 128:(oc + 1) * 128], s.ap[:, k, 0:n], k == 0, k == KC - 1,
                            [s, Wp[k // 4]], [p])
                self.stt("dve", xt.ap[:, oc, 0:n], p.ap[:, 0:n], self.mods.ap[:, l, g0 + oc, j:j + 1],
                         xt.ap[:, oc, 0:n], ALU.mult, ALU.add, [p, xt, self.mods], [xt])
            self.dma("sp", self.xs[i][:, :, 0:n], xt.ap[:, :, 0:n], [xt], [self.r_xs[i]], xt.key + "s")
            if nxt is not None:
                if kind == "ffn":
                    self.norm_tile(nxt[0], nxt[1], i, n, xt, H, nsq, nrs, npr)
                else:
                    if pending is not None:
                        self.norm_tile(*pending)
                    pending = (nxt[0], nxt[1], i, n, xt, H, nsq, nrs, npr)
        if nxt is not None:
            if pending is not None:
                self.norm_tile(*pending)
            self.h_fused = nxt

    def phase_post(self):
        self.phase()
        xr = Ring([self.alloc("pox%d" % i, [8, 512]) for i in range(2)])
        stg = Ring([self.alloc("post%d" % i, [1024]) for i in range(2)])
        pr = Ring(self.ps[0:4])
        for i in range(8):
            xt = xr.next()
            self.dma("sp", xt.ap, self.xs[i], [self.r_xs[i]], [xt], xt.key)
            for b in range(4):
                st = stg.next()
                for half in range(2):
                    p = pr.next()
                    for kk in range(4):
                        k = half * 4 + kk
                        self.tr(p.ap[:, kk * 128:(kk + 1) * 128], xt.ap[:, k, b * 128:(b + 1) * 128], self.ident_f.ap,
                                [xt, self.ident_f], [p])
                    self.cp("act" if half == 0 else "dve", st.ap[:, half * 512:(half + 1) * 512], p.ap, [p], [st])
                r0 = i * 512 + b * 128
                self.out_toks.append(self.dma("sp", self.out[r0:r0 + 128, :], st.ap, [st], [], st.key))
        kn = self.mk.known["sp"]
        waits = []
        for k, v in self.out_toks:
            if kn.get(k, 0) < v:
                kn[k] = v
                waits.append((k, v))
        self.mk.q["sp"].append((None, waits, None))


def ffn_up(self, l):
    d = self.din
    self.phase()
    H = self.norm_pass(l, 2)
    wup = d["ffn_w_up"][l]
    G = Ring([(self.alloc("Gl%d" % i, [NLAT + 2]), self.alloc("Gc%d" % i, [NCTX + 2])) for i in range(2)])
    for gl, gc in G.t:
        for t in (gl, gc):
            self.memset("pool", t.ap, 0.0, [t])
    T1 = Ring([self.alloc("ft%d" % i, [T]) for i in range(2)])
    stg = Ring([self.alloc("fstg%d" % i, [NT, 512], BF16) for i in range(2)])
    wr = Ring([(self.alloc("fwg%d" % i, [8, 256], BF16), self.alloc("fwv%d" % i, [8, 256], BF16)) for i in range(2)])
    pr = Ring(self.ps[0:6])
    segs = ((0, NLAT), (NLAT, NCTX))

    def gate(c, wg, cc, gl, gc):
        for i, (t0, n) in enumerate(TILES):
            p = pr.next()
            for k in range(8):
                self.mm(p.ap[:, 0:n], wg.ap[:, k, cc * 128:(cc + 1) * 128], H[i].ap[:, k, :], k == 0, k == 7, [wg, H[i]], [p])
            dst, off = (gl, 1 + t0) if i < 8 else (gc, 1)
            self.cp("act" if i % 2 == 0 else "dve", dst.ap[:, off:off + n], p.ap[:, 0:n], [p], [dst])

    def conv(c, gl, gc, t1):
        for (g, (s0, sn)) in zip((gl, gc), segs):
            o = t1.ap[:, s0:s0 + sn]
            self.act(o, g.ap[:, 1:1 + sn], AF.Identity, [g, self.cvw, self.cvb], [t1],
                     scale=self.cvw.ap[:, l, 1, c:c + 1], bias=self.cvb.ap[:, l, c:c + 1])
            self.stt("dve", o, g.ap[:, 0:sn], self.cvw.ap[:, l, 0, c:c + 1], o, ALU.mult, ALU.add, [g, t1, self.cvw], [t1])
            self.stt("dve", o, g.ap[:, 2:2 + sn], self.cvw.ap[:, l, 2, c:c + 1], o, ALU.mult, ALU.add, [g, t1, self.cvw], [t1])
            self.act(o, o, AF.Silu, [t1], [t1])

    def val(c, wv, cc, t1):
        st = stg.next()
        for i, (t0, n) in enumerate(TILES):
            p = pr.next()
            for k in range(8):
                self.mm(p.ap[:, 0:n], wv.ap[:, k, cc * 128:(cc + 1) * 128], H[i].ap[:, k, :], k == 0, k == 7, [wv, H[i]], [p])
            self.tt("dve", st.ap[:, i, 0:n], p.ap[:, 0:n], t1.ap[:, t0:t0 + n], ALU.mult, [p, t1], [st])
        self.dma("sp", self.As[:, :, c, :].rearrange("i p t -> p i t"), st.ap, [st], self.r_As, st.key)

    wcur = {}

    def getw(c):
        cb = c // 2
        if cb not in wcur:
            wg, wv = wr.next()
            self.dma("pool", wg.ap, wup[:, cb * 256:(cb + 1) * 256].rearrange("(k p) c -> p k c", p=128), [], [wg], wg.key)
            self.dma("pool", wv.ap, wup[:, DFF + cb * 256:DFF + (cb + 1) * 256].rearrange("(k p) c -> p k c", p=128), [], [wv], wv.key)
            wcur[cb] = (wg, wv)
        return wcur[cb] + (c % 2,)

    gbuf = {}
    wg, wv, cc = getw(0)
    gbuf[0] = G.next()
    gate(0, wg, cc, *gbuf[0])
    for c in range(NFC):
        if c + 1 < NFC:
            wg2, wv2, cc2 = getw(c + 1)
            gbuf[c + 1] = G.next()
            gate(c + 1, wg2, cc2, *gbuf[c + 1])
        t1 = T1.next()
        conv(c, gbuf[c][0], gbuf[c][1], t1)
        wg, wv, cc = getw(c)
        val(c, wv, cc, t1)


Prog.ffn_up = ffn_up


def even_mixer(self, l):
    e = l // 2
    d = self.din
    self.phase()
    H = self.norm_pass(l, 1)
    win = d["ev_w_in"][e]
    CH = 64
    NCH = T // CH
    lb = self.alloc("lb", [2, 4]); oml = self.alloc("oml", [2, 4])
    if e == 0:
        self.memset("pool", lb.ap, 0.0, [lb])
    else:
        lg = self.alloc("lblog", [2, 2, 4])
        self.dma("sp", lg.ap, d["ev_lb"], [], [lg], lg.key)
        self.tt("dve", lb.ap, lg.ap[:, 1], lg.ap[:, 0], ALU.subtract, [lg], [lb])
        self.act(lb.ap, lb.ap, AF.Sigmoid, [lb], [lb])
    self.ts("dve", oml.ap, lb.ap, -1.0, 1.0, ALU.mult, ALU.add, [lb], [oml])
    onw = self.alloc("onw", [2]); self.dma("sp", onw.ap, d["ev_onorm_w"], [], [onw], onw.key)
    mask01 = self.alloc("mask01", [512])
    self.memset("pool", mask01.ap, 1.0, [mask01])
    self.memset("pool", mask01.ap.rearrange("p (c t) -> p c t", t=CH)[:, :, 0:1], 0.0, [mask01])
    masks = []
    for nm, pat, cm in (("mskF", 1, -1), ("mskB", -1, 1)):
        m = self.alloc(nm, [CH], parts=CH)
        self.memset("pool", m.ap, 1.0, [m])
        mp = m.ap
        self.mk.op("pool", lambda en, mp=mp, pat=pat, cm=cm: en.affine_select(out=mp, in_=mp, pattern=[[pat, CH]], compare_op=ALU.is_ge,
                                                                             fill=0.0, base=0, channel_multiplier=cm), [m.res], [m.res])
        masks.append(m)
    hm = []
    for nm, lo, hi in (("hmF", 1.0, 0.0), ("hmB", 0.0, 1.0)):
        m = self.alloc(nm, [512], BF16)
        m3 = m.ap.rearrange("p (c t) -> p c t", t=CH)
        self.memset("pool", m3[:, :, 0:CH // 2], lo, [m])
        self.memset("pool", m3[:, :, CH // 2:CH], hi, [m])
        hm.append(m)
    vnw = self.alloc("vnw", [512]); bsb = self.alloc("bsb", [512])
    self.dma("sp", vnw.ap, d["ev_vnorm_w"][0:1, e * 512:(e + 1) * 512].partition_broadcast(128), [], [vnw], vnw.key)
    self.dma("sp", bsb.ap, d["ev_bs"][0:1, e * 512:(e + 1) * 512].partition_broadcast(128),
             [], [bsb], bsb.key)
    mark_gla = self.a_off
    W5 = self.alloc("W5", [5, 8, 128], BF16)
    Qt = [self.alloc("Qt%d" % i, [T], BF16) for i in range(2)]
    Kt = [self.alloc("Kt%d" % i, [T], BF16) for i in range(2)]
    Qr = [[Res("Qt%d_%d" % (dr, i)) for i in range(NT)] for dr in range(2)]
    Kr = [[Res("Kt%d_%d" % (dr, i)) for i in range(NT)] for dr in range(2)]
    V = self.alloc("V", [NCH, 128], BF16, parts=CH)
    Vr = [Res("V%d" % i) for i in range(NT)]
    O = [self.alloc("O%d" % i, [T], BF16) for i in range(2)]
    Or = [[Res("O%d_%d" % (dr, i)) for i in range(NT)] for dr in range(2)]
    E = [[self.alloc("E%d_%d" % (dr, q), [NCH]) for q in range(3)] for dr in range(2)]
    S = [self.alloc("S%d" % i, [128]) for i in range(2)]
    Sb = [self.alloc("Sb%d" % i, [128], BF16) for i in range(2)]
    KtT = Ring([self.alloc("KtT%d" % i, [8, 128], BF16, parts=CH) for i in range(2)])
    AT = Ring([self.alloc("AT%d" % i, [8, CH], BF16, parts=CH) for i in range(2)])
    KZ = Ring([self.alloc("KZ%d" % i, [512], BF16) for i in range(2)])
    tq = Ring([self.alloc("tq%d" % i, [512]) for i in range(2)])
    tmps = [Ring([self.alloc("tm%d_%d" % (a, i), [512]) for i in range(2 if a < 4 else 1)]) for a in range(6)]
    t8 = Ring([self.alloc("t8_%d" % i, [8]) for i in range(4)])

    def bview(t):
        v = Tl(t.ap.bitcast(BF16)[:, 0:512], t.res.name + "_bv")
        v.res = t.res
        return v

    KE = Ring([bview(t) for t in tmps[2].t])
    QI = Ring([bview(t) for t in tmps[3].t])
    stg = Ring([self.alloc("estg%d" % i, [512], BF16) for i in range(2)])
    sqr = Ring([self.alloc("esq%d" % i, [512], BF16) for i in range(2)])
    pr = Ring(self.ps[0:4])
    pa = Ring(self.ps[4:6])
    pd = Ring(self.ps[6:8])

    class R1:
        def __init__(s, res):
            s.res = res

    for hd in range(4):
        for b, c0 in enumerate((hd * 128, 512 + hd * 128, 1024 + hd * 128, 1536 + hd * 128, 2048 + hd * 128)):
            self.dma("pool", W5.ap[:, b], win[:, c0:c0 + 128].rearrange("(k p) c -> p k c", p=128), [], [W5], W5.key)
        for i, (t0, n) in enumerate(TILES):
            nc_ = n // CH
            ch0 = t0 // CH
            pq = pr.next()
            for k in range(8):
                self.mm(pq.ap[:, 0:n], W5.ap[:, 0, k, :], H[i].ap[:, k, :], k == 0, k == 7, [W5, H[i]], [pq])
            qf = tq.next()
            self.act(qf.ap[:, 0:n], pq.ap[:, 0:n], AF.Silu, [pq], [qf])
            for dr in range(2):
                pz = pr.next()
                for k in range(8):
                    self.mm(pz.ap[:, 0:n], W5.ap[:, 2 + dr, k, :], H[i].ap[:, k, :], k == 0, k == 7, [W5, H[i]], [pz])
                f, g, bt, dt, eq, ek = [r.next() for r in tmps]
                fa, ga, ba, da, eqa, eka = [x.ap[:, 0:n] for x in (f, g, bt, dt, eq, ek)]
                self.act(fa, pz.ap[:, 0:n], AF.Sigmoid, [pz], [f])
                self.ts("dve", fa, fa, oml.ap[:, dr, hd:hd + 1], lb.ap[:, dr, hd:hd + 1], ALU.mult, ALU.add, [f, oml, lb], [f])
                self.ts("pool", fa, fa, F_MIN, 1.0, ALU.max, ALU.mult, [f], [f])
                self.act(ga, fa, AF.Ln, [f], [g])
                self.ts("pool", fa, fa, -1.0, 1.0, ALU.mult, ALU.add, [f, g], [f])
                m01 = mask01.ap[:, 0:n]
                self.mk.op("dve", lambda en, ba=ba, m01=m01, ga=ga: en.tensor_tensor_scan(out=ba, data0=m01, data1=ga, initial=0.0,
                                                                                         op0=ALU.mult, op1=ALU.add),
                           [mask01.res, g.res], [bt.res])
                b3 = ba.rearrange("p (c t) -> p c t", t=CH)
                if dr == 0:
                    c3 = b3
                    ctile = bt
                else:
                    self.tt("pool", ga, ga, ba, ALU.subtract, [g, bt], [g])
                    c3 = ga.rearrange("p (c t) -> p c t", t=CH)
                    ctile = g
                tot = b3[:, :, CH - 1:CH]
                cr = c3[:, :, CH // 2 - 1:CH // 2]
                self.tt("dve", da.rearrange("p (c t) -> p c t", t=CH), c3, cr.to_broadcast([128, nc_, CH]), ALU.subtract,
                        [ctile], [dt])
                e1, e2, e3 = [x.ap[:, ch0:ch0 + nc_].unsqueeze(2) for x in E[dr]]
                t8a = t8.next()
                t8v = t8a.ap[:, 0:nc_].unsqueeze(2)
                self.act(e1, tot, AF.Exp, [bt], [E[dr][0]])
                if dr == 0:
                    self.tt("dve", t8v, tot, cr, ALU.subtract, [bt, ctile], [t8a])
                    self.act(e2, t8v, AF.Exp, [t8a], [E[dr][1]])
                    self.act(e3, cr, AF.Exp, [ctile], [E[dr][2]])
                else:
                    self.act(e2, cr, AF.Exp, [ctile], [E[dr][1]], scale=-1.0)
                    self.tt("dve", t8v, tot, cr, ALU.add, [bt, ctile], [t8a])
                    self.act(e3, t8v, AF.Exp, [t8a], [E[dr][2]])
                self.act(eqa, da, AF.Exp, [dt], [eq])
                self.act(eka, da, AF.Exp, [dt], [ek], scale=-1.0)
                self.tt("pool", Qt[dr].ap[:, t0:t0 + n], qf.ap[:, 0:n], eqa, ALU.mult, [qf, eq], [R1(Qr[dr][i])])
                self.tt("dve", Kt[dr].ap[:, t0:t0 + n], fa, eka, ALU.mult, [f, ek], [R1(Kr[dr][i])])
            for h2 in range(0, nc_, 4):
                pv = pr.next()
                for q in range(4):
                    for k in range(8):
                        self.mm(pv.ap[0:CH, q * 128:(q + 1) * 128], H[i].ap[:, k, (h2 + q) * CH:(h2 + q + 1) * CH], W5.ap[:, 1, k, :],
                                k == 0, k == 7, [W5, H[i]], [pv])
                self.cp("act", V.ap[:, ch0 + h2:ch0 + h2 + 4, :], pv.ap[0:CH, :].rearrange("p (a b) -> p a b", a=4),
                        [pv], [R1(Vr[i])])
        for dr in range(2):
            self.memset("pool", Sb[dr].ap, 0.0, [Sb[dr]])
        order = ([8] + list(range(8)), [8] + list(range(7, -1, -1)))
        for step in range(NT):
            for dr in range(2):
                i = order[dr][step]
                t0, n = TILES[i]
                nc_ = n // CH
                ch0 = t0 // CH
                qr_, kr_, vr_ = R1(Qr[dr][i]), R1(Kr[dr][i]), R1(Vr[i])
                ke = KE.next(); qi_ = QI.next()
                k3 = Kt[dr].ap[:, t0:t0 + n].rearrange("p (c t) -> p c t", t=CH)
                q3 = Qt[dr].ap[:, t0:t0 + n].rearrange("p (c t) -> p c t", t=CH)
                self.tt("pool", ke.ap[:, 0:n].rearrange("p (c t) -> p c t", t=CH), k3,
                        E[dr][1].ap[:, ch0:ch0 + nc_].unsqueeze(2).to_broadcast([128, nc_, CH]), ALU.mult, [kr_, E[dr][1]], [ke])
                self.tt("pool", qi_.ap[:, 0:n].rearrange("p (c t) -> p c t", t=CH), q3,
                        E[dr][2].ap[:, ch0:ch0 + nc_].unsqueeze(2).to_broadcast([128, nc_, CH]), ALU.mult, [qr_, E[dr][2]], [qi_])
                ptr = pa.next()
                ptb = ptr.ap.bitcast(BF16)
                for q in range(nc_):
                    self.tr(ptb[0:CH, q * 128:(q + 1) * 128], ke.ap[:, q * CH:(q + 1) * CH], self.ident_b.ap,
                            [ke, self.ident_b], [ptr])
                ktt = KtT.next()
                self.cp("act", ktt.ap[:, 0:nc_, :], ptb[0:CH, 0:nc_ * 128].rearrange("p (a b) -> p a b", a=nc_), [ptr], [ktt])
                kz = KZ.next()
                self.tt("pool", kz.ap[:, 0:n], Kt[dr].ap[:, t0:t0 + n], hm[dr].ap[:, 0:n], ALU.mult, [kr_, hm[dr]], [kz])
                psc = pa.next()
                hf = CH // 2
                for q in range(nc_):
                    sl = slice(t0 + q * CH, t0 + (q + 1) * CH)
                    sl0 = slice(t0 + q * CH, t0 + q * CH + hf)
                    sl1 = slice(t0 + q * CH + hf, t0 + (q + 1) * CH)
                    kzc = kz.ap[:, q * CH:(q + 1) * CH]
                    kfc = Kt[dr].ap[:, sl]
                    self.mm(psc.ap[0:CH, q * CH:q * CH + hf], kzc if dr == 0 else kfc, Qt[dr].ap[:, sl0], True, True,
                            [kr_, qr_, kz], [psc])
                    self.mm(psc.ap[0:CH, q * CH + hf:(q + 1) * CH], kfc if dr == 0 else kzc, Qt[dr].ap[:, sl1], True, True,
                            [kr_, qr_, kz], [psc])
                at = AT.next()
                self.tt("dve", at.ap[:, 0:nc_, :], psc.ap[0:CH, 0:nc_ * CH].rearrange("p (a b) -> p a b", a=nc_),
                        masks[dr].ap.unsqueeze(1).to_broadcast([CH, nc_, CH]), ALU.mult, [psc, masks[dr]], [at])
                po = pr.next()
                qs = list(range(nc_))
                if dr == 1:
                    qs = qs[::-1]
                for q in qs:
                    ch = ch0 + q
                    self.mm(po.ap[:, q * CH:(q + 1) * CH], V.ap[:, ch, :], at.ap[:, q, :], True, False, [vr_, at], [po])
                    self.mm(po.ap[:, q * CH:(q + 1) * CH], Sb[dr].ap, qi_.ap[:, q * CH:(q + 1) * CH], False, True, [Sb[dr], qi_], [po])
                    pds = pd.next()
                    self.mm(pds.ap[:, 0:128], ktt.ap[:, q, :], V.ap[:, ch, :], True, True, [ktt, vr_], [pds])
                    self.stt("dve", Sb[dr].ap, Sb[dr].ap, E[dr][0].ap[:, ch:ch + 1], pds.ap[:, 0:128], ALU.mult, ALU.add,
                             [pds, Sb[dr], E[dr][0]], [Sb[dr]])
                self.cp("act", O[dr].ap[:, t0:t0 + n], po.ap[:, 0:n], [po], [R1(Or[dr][i])])
        for i, (t0, n) in enumerate(TILES):
            osum = tmps[0].next(); sg = tmps[1].next(); rs = tmps[2].next()
            self.tt("pool", osum.ap[:, 0:n], O[0].ap[:, t0:t0 + n], O[1].ap[:, t0:t0 + n], ALU.add,
                    [R1(Or[0][i]), R1(Or[1][i])], [osum])
            sq = sqr.next()
            self.act(sq.ap[:, 0:n], osum.ap[:, 0:n], AF.Square, [osum], [sq])
            pn = pa.next()
            self.mm(pn.ap[:, 0:n], self.ones_b.ap, sq.ap[:, 0:n], True, True, [sq, self.ones_b], [pn])
            self.rstd(rs.ap[:, 0:n], pn.ap[:, 0:n], 128, [pn], [rs])
            pg = pr.next()
            for k in range(8):
                self.mm(pg.ap[:, 0:n], W5.ap[:, 4, k, :], H[i].ap[:, k, :], k == 0, k == 7, [W5, H[i]], [pg])
            self.act(sg.ap[:, 0:n], pg.ap[:, 0:n], AF.Silu, [pg], [sg])
            self.tt("dve", osum.ap[:, 0:n], osum.ap[:, 0:n], rs.ap[:, 0:n], ALU.mult, [osum, rs], [osum])
            st = stg.next()
            self.stt("dve", st.ap[:, 0:n], osum.ap[:, 0:n], onw.ap[:, e:e + 1], sg.ap[:, 0:n], ALU.mult, ALU.mult,
                     [osum, onw, sg], [st])
            self.dma("sp", self.Ms[i][:, hd, 0:n], st.ap[:, 0:n], [st], [self.r_Ms[i]], st.key)
    self.barrier()
    self.a_off = mark_gla
    tmps = [Ring([self.alloc("cm%d_%d" % (a, i), [512]) for i in range(2)]) for a in range(4)]
    stg = Ring([self.alloc("cstg%d" % i, [512], BF16) for i in range(2)])
    Wuv = Ring([self.alloc("Wuv%d" % i, [2, 8, 128], BF16) for i in range(2)])
    wsr = Ring([self.alloc("wsT%d" % i, [128], BF16) for i in range(2)])
    vnb = Ring([self.alloc("vnb%d" % i, [4, 128], BF16) for i in range(2)])
    ss4 = Ring([self.alloc("ss4_%d" % i, [4]) for i in range(2)])
    for g in range(4):
        w = Wuv.next(); wsT = wsr.next()
        self.dma("pool", w.ap[:, 0], win[:, 2560 + g * 128:2560 + (g + 1) * 128].rearrange("(k p) c -> p k c", p=128), [], [w], w.key)
        self.dma("pool", w.ap[:, 1], win[:, 3072 + g * 128:3072 + (g + 1) * 128].rearrange("(k p) c -> p k c", p=128), [], [w], w.key)
        self.dma("pool", wsT.ap, d["ev_wsT"][e, g], [], [wsT], wsT.key)
        for i, (t0, n) in enumerate(TILES):
            nb = n // 128
            pu = pr.next()
            for k in range(8):
                self.mm(pu.ap[:, 0:n], w.ap[:, 0, k, :], H[i].ap[:, k, :], k == 0, k == 7, [w, H[i]], [pu])
            U = tmps[0].next()
            self.act(U.ap[:, 0:n], pu.ap[:, 0:n], AF.Gelu_apprx_tanh, [pu], [U])
            pv = pr.next()
            for b in range(nb):
                for k in range(8):
                    self.mm(pv.ap[:, b * 128:(b + 1) * 128], H[i].ap[:, k, b * 128:(b + 1) * 128], w.ap[:, 1, k, :],
                            k == 0, k == 7, [w, H[i]], [pv])
            gv = tmps[1].next(); sqv = tmps[2].next(); s4 = ss4.next()
            gv3 = gv.ap[:, 0:n].rearrange("p (a b) -> p a b", b=128)
            sq3 = sqv.ap[:, 0:n].rearrange("p (a b) -> p a b", b=128)
            self.act(gv.ap[:, 0:n], pv.ap[:, 0:n], AF.Gelu_apprx_tanh, [pv], [gv])
            self.tt("pool", sqv.ap[:, 0:n], gv.ap[:, 0:n], gv.ap[:, 0:n], ALU.mult, [gv], [sqv])
            s4a = s4.ap[:, 0:nb]
            self.mk.op("dve", lambda en, s4a=s4a, sq3=sq3: en.tensor_reduce(out=s4a, in_=sq3, axis=AX.X, op=ALU.add),
                       [sqv.res], [s4.res])
            self.rstd(s4a, s4a, 128, [s4], [s4])
            self.tt("dve", gv3, gv3, s4a.unsqueeze(2).to_broadcast([128, nb, 128]), ALU.mult, [gv, s4], [gv])
            vn = vnb.next()
            self.tt("pool", vn.ap[:, 0:nb, :], gv3, vnw.ap[:, g * 128:(g + 1) * 128].unsqueeze(1).to_broadcast([128, nb, 128]),
                    ALU.mult, [gv, vnw], [vn])
            psv = pr.next()
            for b in range(nb):
                self.mm(psv.ap[:, b * 128:(b + 1) * 128], vn.ap[:, b, :], wsT.ap, True, True, [vn, wsT], [psv])
            tsv = tmps[3].next()
            self.tt("dve", tsv.ap[:, 0:n].rearrange("p (a b) -> p a b", b=128), psv.ap[:, 0:n].rearrange("p (a b) -> p a b", b=128),
                    bsb.ap[:, g * 128:(g + 1) * 128].unsqueeze(1).to_broadcast([128, nb, 128]), ALU.add, [psv, bsb], [tsv])
            st = stg.next()
            self.tt("pool", st.ap[:, 0:n], tsv.ap[:, 0:n], U.ap[:, 0:n], ALU.mult, [tsv, U], [st])
            self.dma("sp", self.Ms[i][:, 4 + g, 0:n], st.ap[:, 0:n], [st], [self.r_Ms[i]], st.key)


Prog.even_mixer = even_mixer


def odd_mixer(self, l):
    o = l // 2
    d = self.din
    self.phase()

    class R1:
        def __init__(s, res):
            s.res = res

    XT = self.alloc("XT", [2, T], BF16)
    QL = self.alloc("QL", [3, T], BF16)
    KVL = self.alloc("KVL", [2, T], BF16)
    KPE = self.alloc("KPE", [T], BF16, parts=64)
    XTr = [Res("XT%d" % i) for i in range(NT)]
    QLr = [Res("QL%d" % i) for i in range(NT)]
    KVLr = [Res("KVL%d" % i) for i in range(NT)]
    KPEr = [Res("KPE%d" % i) for i in range(NT)]
    qaw = self.alloc("qaw", [2, 3]); kvaw = self.alloc("kvaw", [2, 2])
    qnn = self.alloc("qnn", [2]); knn = self.alloc("knn", [2])
    qnp = self.alloc("qnp", [2], parts=64); knp = self.alloc("knp", [2], parts=64)
    rot = self.alloc("rot", [64], BF16, parts=64)
    for t_, nm in ((qaw, "od_qa_w"), (kvaw, "od_kva_w"), (qnn, "od_qn_nope"), (knn, "od_kn_nope"), (qnp, "od_qn_pe"), (knp, "od_kn_pe")):
        self.dma("sp", t_.ap, d[nm], [], [t_], t_.key)
    self.dma("pool", rot.ap, d["rope_rot"], [], [rot], rot.key)
    sqr = Ring([self.alloc("osq%d" % i, [3, 512], BF16) for i in range(2)])
    rsr = Ring([self.alloc("ors%d" % i, [512]) for i in range(2)])
    cosr = Ring([(self.alloc("cos%d" % i, [512], parts=64), self.alloc("sin%d" % i, [512], parts=64)) for i in range(2)])
    rtmp = Ring([(self.alloc("rpf%d" % i, [512], parts=64), self.alloc("rpb%d" % i, [512], BF16, parts=64),
                  self.alloc("rpc%d" % i, [512], parts=64)) for i in range(2)])
    pmark = self.a_off
    pr = Ring(self.ps[0:3])
    pn_ring = Ring([self.ps[3]])

    def rms_group(pss, parts, n, nfeat, wcols, outs, out_tiles, wt):
        sq = sqr.next()
        for c, p in enumerate(pss):
            self.act(sq.ap[0:parts, c, 0:n], p.ap[0:parts, 0:n], AF.Square, [p], [sq])
        pn = pn_ring.next()
        for c in range(len(pss)):
            self.mm(pn.ap[:, 0:n], self.ones_b.ap[0:parts, :], sq.ap[0:parts, c, 0:n], c == 0, c == len(pss) - 1,
                    [sq, self.ones_b], [pn])
        rs = rsr.next()
        self.rstd(rs.ap[:, 0:n], pn.ap[:, 0:n], nfeat, [pn], [rs])
        for c, p in enumerate(pss):
            self.stt("dve", outs[c], p.ap[0:parts, 0:n], wcols[c], rs.ap[0:parts, 0:n], ALU.mult, ALU.mult,
                     [p, rs, wt], [out_tiles[c]])

    def rope(p, wcol, t0, n, out_ap, out_tile, wt):
        pf, pb, pc = rtmp.next()
        rms_group([p], 64, n, 64, [wcol], [pf.ap[:, 0:n]], [pf], wt)
        cs, sn = cosr.next()
        self.dma("sp", cs.ap[:, 0:n], d["rope_cos"][:, t0:t0 + n], [], [cs], cs.key)
        self.dma("sp", sn.ap[:, 0:n], d["rope_sin"][:, t0:t0 + n], [], [sn], sn.key)
        self.cp("act", pb.ap[:, 0:n], pf.ap[:, 0:n], [pf], [pb])
        p2 = pr.next()
        self.mm(p2.ap[0:64, 0:n], rot.ap, pb.ap[:, 0:n], True, True, [rot, pb], [p2])
        self.tt("dve", pc.ap[:, 0:n], p2.ap[0:64, 0:n], sn.ap[:, 0:n], ALU.mult, [p2, sn], [pc])
        self.tt("pool", pf.ap[:, 0:n], pf.ap[:, 0:n], cs.ap[:, 0:n], ALU.mult, [pf, cs], [pf])
        self.tt("pool", out_ap, pf.ap[:, 0:n], pc.ap[:, 0:n], ALU.add, [pf, pc], [out_tile])

    H = self.norm_pass(l, 1, nb=1)
    Win = self.alloc("oWin", [8, OD_IN], BF16)
    wod = d["od_w_in"][o]
    self.dma("pool", Win.ap[:, 0:4], wod[0:512, :].rearrange("(k p) c -> p k c", p=128), [], [Win], Win.key)
    self.dma("pool", Win.ap[:, 4:8], wod[512:1024, :].rearrange("(k p) c -> p k c", p=128), [], [Win], Win.key)

    def proj(c0, m, i, n):
        p = pr.next()
        for k in range(8):
            self.mm(p.ap[0:m, 0:n], Win.ap[:, k, c0:c0 + m], H[i].ap[:, k, :], k == 0, k == 7, [Win, H[i]], [p])
        return p

    for i, (t0, n) in enumerate(TILES):
        for cc in range(2):
            p = proj(cc * 128, 128, i, n)
            self.cp("act" if cc == 0 else "dve", XT.ap[:, cc, t0:t0 + n], p.ap[:, 0:n], [p], [R1(XTr[i])])
        pss = [proj(256 + c * 128, 128, i, n) for c in range(3)]
        rms_group(pss, 128, n, 384, [qaw.ap[:, o, c:c + 1] for c in range(3)],
                  [QL.ap[:, c, t0:t0 + n] for c in range(3)], [R1(QLr[i])] * 3, qaw)
        pss = [proj(640 + c * 128, 128, i, n) for c in range(2)]
        rms_group(pss, 128, n, 256, [kvaw.ap[:, o, c:c + 1] for c in range(2)],
                  [KVL.ap[:, c, t0:t0 + n] for c in range(2)], [R1(KVLr[i])] * 2, kvaw)
        p = proj(896, 64, i, n)
        if i < 8:
            rope(p, knp.ap[:, o:o + 1], t0, n, KPE.ap[:, t0:t0 + n], R1(KPEr[i]), knp)
        else:
            rms_group([p], 64, n, 64, [knp.ap[:, o:o + 1]], [KPE.ap[:, t0:t0 + n]], [R1(KPEr[i])], knp)
    self.barrier()
    self.a_off = pmark
    self.free_H()
    ccb = self.alloc("ccb", [128], BF16); ssb = self.alloc("ssb", [128], BF16)
    self.dma("sp", ccb.ap, d["dft_cc"], [], [ccb], ccb.key)
    self.dma("sp", ssb.ap, d["dft_ss"], [], [ssb], ssb.key)
    XCS = self.alloc("XCS", [T // 128, 512], BF16)
    XCSr = Res("XCS")
    for tb in range(T // 128):
        p = pr.next()
        for cc in range(2):
            xt_ = XT.ap[:, cc, tb * 128:(tb + 1) * 128]
            self.mm(p.ap[:, cc * 128:(cc + 1) * 128], xt_, ccb.ap, True, True, [R1(XTr[tb // 4]), ccb], [p])
            self.mm(p.ap[:, 256 + cc * 128:256 + (cc + 1) * 128], xt_, ssb.ap, True, True, [R1(XTr[tb // 4]), ssb], [p])
        self.cp("act" if tb % 2 == 0 else "dve", XCS.ap[:, tb, :], p.ap, [p], [R1(XCSr)])
    tabr = Ring([self.alloc("tab%d" % i, [8, 512], BF16) for i in range(4)])
    fstg = Ring([self.alloc("fst%d" % i, [512], BF16) for i in range(2)])
    pf2 = Ring(self.ps[4:8])
    for ft in range(8):
        p0, p1 = pf2.next(), pf2.next()
        step = 0
        for tname, so in (("dft_c", 0), ("dft_ns", 256)):
            for pc_ in range(4):
                tab = tabr.next()
                self.dma("sp", tab.ap, d[tname][pc_ * 1024:(pc_ + 1) * 1024, ft * 512:(ft + 1) * 512].rearrange("(b p) f -> p b f", p=128),
                         [], [tab], tab.key)
                for b in range(8):
                    tb = pc_ * 8 + b
                    for cc, p in ((0, p0), (1, p1)):
                        self.mm(p.ap, XCS.ap[:, tb, so + cc * 128:so + (cc + 1) * 128], tab.ap[:, b, :], step == 0, step == 63,
                                [R1(XCSr), tab], [p])
                    step += 1
        for cc, p in ((0, p0), (1, p1)):
            st = fstg.next()
            self.act(st.ap, p.ap, AF.Copy, [p], [st], scale=1.0 / 512.0)
            self.dma("sp", self.Ms[ft][:, cc, :], st.ap, [st], [self.r_Ms[ft]], st.key)
    ctab = [self.alloc("ctab%d" % i, [2, NCTX], BF16) for i in range(2)]
    for t_, nm in zip(ctab, ("dftc_c", "dftc_ns")):
        self.dma("sp", t_.ap, d[nm].rearrange("(b p) f -> p b f", p=128), [], [t_], t_.key)
    p0, p1 = pf2.next(), pf2.next()
    step = 0
    for t_, so in zip(ctab, (0, 256)):
        for b in range(2):
            for cc, p in ((0, p0), (1, p1)):
                self.mm(p.ap[:, 0:NCTX], XCS.ap[:, 32 + b, so + cc * 128:so + (cc + 1) * 128], t_.ap[:, b, :], step == 0, step == 3,
                        [R1(XCSr), t_], [p])
            step += 1
    for cc, p in ((0, p0), (1, p1)):
        st = fstg.next()
        self.act(st.ap[:, 0:NCTX], p.ap[:, 0:NCTX], AF.Copy, [p], [st], scale=1.0 / 128.0)
        self.dma("sp", self.Ms[8][:, cc, 0:NCTX], st.ap[:, 0:NCTX], [st], [self.r_Ms[8]], st.key)
    self.barrier()
    self.a_off = pmark
    Qn = self.alloc("Qn", [T], BF16); Kn = self.alloc("Kn", [T], BF16)
    Qp = self.alloc("Qp", [T], BF16, parts=64)
    V = self.alloc("aV", [T // 128, 128], BF16)
    Qnr = [Res("Qn%d" % i) for i in range(NT)]; Knr = [Res("Kn%d" % i) for i in range(NT)]
    Qpr = [Res("Qp%d" % i) for i in range(NT)]; Vr = [Res("aV%d" % i) for i in range(NT)]
    wqn = self.alloc("wqn", [3, 128], BF16); wqp = self.alloc("wqp", [3, 64], BF16)
    wkn = self.alloc("wkn", [2, 128], BF16); wv = self.alloc("wv", [2, 128], BF16)
    PT = Ring([self.alloc("PT%d" % i, [512], BF16) for i in range(4)])
    rden = Ring([self.alloc("rden%d" % i, [512]) for i in range(2)])
    accr = Ring([self.alloc("acc%d" % i, [512]) for i in range(2)])
    ones_f = self.alloc("ones_f", [128])
    self.memset("pool", ones_f.ap, 1.0, [ones_f])
    astg = Ring([self.alloc("astg%d" % i, [512], BF16) for i in range(2)])
    po_r = Ring(self.ps[4:6]); pd_r = Ring(self.ps[6:8])
    wqb, wkvb = d["od_w_qb"][o], d["od_w_kvb"][o]
    SC = 192.0 ** -0.5
    for h in range(6):
        self.dma("pool", wqn.ap, wqb[:, h * 192:h * 192 + 128].rearrange("(k p) c -> p k c", p=128), [], [wqn], wqn.key)
        self.dma("pool", wqp.ap, wqb[:, h * 192 + 128:(h + 1) * 192].rearrange("(k p) c -> p k c", p=128), [], [wqp], wqp.key)
        self.dma("pool", wkn.ap, wkvb[:, h * 256:h * 256 + 128].rearrange("(k p) c -> p k c", p=128), [], [wkn], wkn.key)
        self.dma("pool", wv.ap, wkvb[:, h * 256 + 128:(h + 1) * 256].rearrange("(k p) c -> p k c", p=128), [], [wv], wv.key)
        for i, (t0, n) in enumerate(TILES):
            p = pr.next()
            for k in range(3):
                self.mm(p.ap[:, 0:n], wqn.ap[:, k, :], QL.ap[:, k, t0:t0 + n], k == 0, k == 2, [wqn, R1(QLr[i])], [p])
            rms_group([p], 128, n, 128, [qnn.ap[:, o:o + 1]], [Qn.ap[:, t0:t0 + n]], [R1(Qnr[i])], qnn)
            p = pr.next()
            for k in range(3):
                self.mm(p.ap[0:64, 0:n], wqp.ap[:, k, :], QL.ap[:, k, t0:t0 + n], k == 0, k == 2, [wqp, R1(QLr[i])], [p])
            if i < 8:
                rope(p, qnp.ap[:, o:o + 1], t0, n, Qp.ap[:, t0:t0 + n], R1(Qpr[i]), qnp)
            else:
                rms_group([p], 64, n, 64, [qnp.ap[:, o:o + 1]], [Qp.ap[:, t0:t0 + n]], [R1(Qpr[i])], qnp)
            p = pr.next()
            for k in range(2):
                self.mm(p.ap[:, 0:n], wkn.ap[:, k, :], KVL.ap[:, k, t0:t0 + n], k == 0, k == 1, [wkn, R1(KVLr[i])], [p])
            rms_group([p], 128, n, 128, [knn.ap[:, o:o + 1]], [Kn.ap[:, t0:t0 + n]], [R1(Knr[i])], knn)
            p = pr.next()
            nb = n // 128
            for b in range(nb):
                for k in range(2):
                    self.mm(p.ap[:, b * 128:(b + 1) * 128], KVL.ap[:, k, t0 + b * 128:t0 + (b + 1) * 128], wv.ap[:, k, :],
                            k == 0, k == 1, [wv, R1(KVLr[i])], [p])
            self.cp("act", V.ap[:, t0 // 128:t0 // 128 + nb, :], p.ap[:, 0:n].rearrange("p (a b) -> p a b", b=128), [p], [R1(Vr[i])])
        for qi, (q0, nq) in enumerate(TILES):
            keys = list(range(T // 128)) if qi < 8 else [32, 33]
            po, pdn = po_r.next(), pd_r.next()
            acc = accr.next()
            pts = {}

            def qk(kb):
                ps_ = pr.next()
                ksl = slice(kb * 128, (kb + 1) * 128)
                self.mm(ps_.ap[:, 0:nq], Kn.ap[:, ksl], Qn.ap[:, q0:q0 + nq], True, False,
                        [R1(Knr[kb // 4]), R1(Qnr[qi])], [ps_])
                self.mm(ps_.ap[:, 0:nq], KPE.ap[:, ksl], Qp.ap[:, q0:q0 + nq], False, True,
                        [R1(KPEr[kb // 4]), R1(Qpr[qi])], [ps_])
                pt = PT.next()
                self.act(pt.ap[:, 0:nq], ps_.ap[:, 0:nq], AF.Exp, [ps_], [pt], scale=SC)
                pts[kb] = pt

            def pv(idx):
                kb = keys[idx]
                pt = pts.pop(kb)
                self.mm(po.ap[:, 0:nq], V.ap[:, kb, :], pt.ap[:, 0:nq], idx == 0, idx == len(keys) - 1, [R1(Vr[kb // 4]), pt], [po])
                if idx == 0:
                    self.cp("pool", acc.ap[:, 0:nq], pt.ap[:, 0:nq], [pt], [acc])
                else:
                    self.tt("pool", acc.ap[:, 0:nq], acc.ap[:, 0:nq], pt.ap[:, 0:nq], ALU.add, [acc, pt], [acc])

            LOOK = 2
            for idx in range(len(keys) + LOOK):
                if idx < len(keys):
                    qk(keys[idx])
                if idx >= LOOK:
                    pv(idx - LOOK)
            self.mm(pdn.ap[:, 0:nq], ones_f.ap, acc.ap[:, 0:nq], True, True, [ones_f, acc], [pdn])
            rd = rden.next()
            self.mk.op("dve", lambda en, a=rd.ap[:, 0:nq], b=pdn.ap[:, 0:nq]: en.reciprocal(out=a, in_=b), [pdn.res], [rd.res])
            st = astg.next()
            self.tt("dve", st.ap[:, 0:nq], po.ap[:, 0:nq], rd.ap[:, 0:nq], ALU.mult, [po, rd], [st])
            self.dma("sp", self.Ms[qi][:, 2 + h, 0:nq], st.ap[:, 0:nq], [st], [self.r_Ms[qi]], st.key)


Prog.odd_mixer = odd_mixer


_PROG = {}


def _consts():
    f32 = np.float32
    bf = ml_dtypes.bfloat16
    rows = NLAT // 64
    row = np.repeat(np.arange(rows), 64)
    col = np.tile(np.arange(64), rows)
    inv_freq = (10000.0 ** (-np.arange(0, 32, 2, dtype=f32) / f32(32))).astype(f32)
    ang = np.stack([row, col], axis=-1).astype(f32)[:, :, None] * inv_freq
    cos = np.cos(ang).astype(f32)
    sin = np.sin(ang).astype(f32)
    cos_f = np.repeat(cos[:, :, None, :], 2, axis=2).reshape(NLAT, 64).T
    sin_f = np.repeat(sin[:, :, None, :], 2, axis=2).reshape(NLAT, 64).T
    R = np.zeros((64, 64), f32)
    for a in range(2):
        for i in range(16):
            d0 = a * 32 + i
            d1 = a * 32 + 16 + i
            R[d1, d0] = -1.0
            R[d0, d1] = 1.0

    def dft(n):
        t = np.arange(n, dtype=np.int64)
        m = (t[:, None] * t[None, :]) % n
        a = 2.0 * np.pi * m.astype(np.float64) / n
        return np.cos(a).astype(bf), (-np.sin(a)).astype(bf)

    c4, ns4 = dft(NLAT)
    cc_, ncs_ = dft(NCTX)
    t = np.arange(64)
    a = 2.0 * np.pi * ((t[:, None] * t[None, :]) % 64) / 64.0
    cc = np.zeros((128, 128), np.float64); ss = np.zeros((128, 128), np.float64)
    for g in range(2):
        cc[g * 64:(g + 1) * 64, g * 64:(g + 1) * 64] = np.cos(a)
        ss[g * 64:(g + 1) * 64, g * 64:(g + 1) * 64] = np.sin(a)
    return dict(rope_cos=np.ascontiguousarray(cos_f), rope_sin=np.ascontiguousarray(sin_f), rope_rot=R,
                dft_c=c4, dft_ns=ns4, dftc_c=cc_, dftc_ns=ncs_, dft_cc=cc.astype(bf), dft_ss=ss.astype(bf))


def _layout(inp):
    f = lambda a: np.ascontiguousarray(np.asarray(a, dtype=np.float32))
    sh = {}
    sh["ada_w"] = f(inp["ada_w"])
    sh["ada_b"] = f(np.asarray(inp["ada_b"]).reshape(DEPTH, 48, 128).transpose(2, 0, 1))
    sh["norm_mix_w"] = f(np.asarray(inp["norm_mix_w"]).reshape(DEPTH, 8, 128).transpose(2, 0, 1))
    sh["norm_ffn_w"] = f(np.asarray(inp["norm_ffn_w"]).reshape(DEPTH, 8, 128).transpose(2, 0, 1))
    sh["ev_w_in"] = f(inp["ev_w_in"])
    sh["ev_lb"] = f(np.asarray(inp["ev_lb_logits"]).reshape(2, 2, 4, 128).transpose(3, 0, 1, 2))
    sh["ev_onorm_w"] = f(np.asarray(inp["ev_onorm_w"]).T)
    sh["ev_vnorm_w"] = f(np.asarray(inp["ev_vnorm_w"]).reshape(1, -1))
    sh["ev_wsT"] = f(np.asarray(inp["ev_ws"]).transpose(0, 1, 3, 2))
    sh["ev_bs"] = f(np.asarray(inp["ev_bs"]).reshape(1, -1))
    sh["ev_w_out"] = f(inp["ev_w_out"])
    sh["od_w_in"] = f(inp["od_w_in"])
    sh["od_qa_w"] = f(np.asarray(inp["od_qa_norm_w"]).reshape(2, 3, 128).transpose(2, 0, 1))
    sh["od_w_qb"] = f(inp["od_w_qb"])
    sh["od_kva_w"] = f(np.asarray(inp["od_kva_norm_w"]).reshape(2, 2, 128).transpose(2, 0, 1))
    sh["od_w_kvb"] = f(inp["od_w_kvb"])
    qn = np.asarray(inp["od_q_norm_w"]); kn = np.asarray(inp["od_k_norm_w"])
    sh["od_qn_nope"] = f(qn[:, :128].T); sh["od_qn_pe"] = f(qn[:, 128:].T)
    sh["od_kn_nope"] = f(kn[:, :128].T); sh["od_kn_pe"] = f(kn[:, 128:].T)
    sh["od_w_out"] = f(inp["od_w_out"])
    sh["ffn_w_up"] = f(inp["ffn_w_up"])
    sh["ffn_conv_w"] = f(np.asarray(inp["ffn_conv_w"]).reshape(DEPTH, 3, NFC, 128).transpose(3, 0, 1, 2))
    sh["ffn_conv_b"] = f(np.asarray(inp["ffn_conv_b"]).reshape(DEPTH, NFC, 128).transpose(2, 0, 1))
    sh["ffn_w_down"] = f(inp["ffn_w_down"])
    sh.update(_consts())
    return sh


def kernel(**inputs):
    nl = int(inputs.pop("_nlayers", DEPTH))
    stop = inputs.pop("_stop", None)
    if (nl, stop) not in _PROG:
        _PROG[(nl, stop)] = Prog(nl, stop)
    prog = _PROG[(nl, stop)]
    sh = _layout(inputs)
    x = np.asarray(inputs["x"], dtype=np.float32)
    c = np.asarray(inputs["c"], dtype=np.float32)
    ctx = np.asarray(inputs["ctx"], dtype=np.float32)
    cc = np.asarray(inputs["c_ctx"], dtype=np.float32)
    B = x.shape[0]
    in_maps = []
    for b in range(B):
        m = dict(sh)
        m["x"] = np.ascontiguousarray(x[b])
        m["ctx"] = np.ascontiguousarray(ctx[b])
        m["cvec"] = np.ascontiguousarray(np.stack([c[b].reshape(8, 128).T, cc.reshape(8, 128).T], axis=-1))
        in_maps.append(m)
    res = run_bass_kernel_spmd(prog.nc, in_maps, core_ids=list(range(B)))
    return np.stack([np.asarray(r["out"], dtype=np.float32) for r in res.results], axis=0)
```
